# Optimizing a Trainium2 kernel written in Bass

```python
import math
import jax, jax.numpy as jnp
from jax import lax
import numpy as np

D_MODEL = 2048
BATCH = 4
SEQ = 8192
DEPTH = 4
DEC_BATCH = 32
DEC_SEQ = 16
PAST_LEN = 2048

CHUNK = 64
N_MIXERS = 2
N_SSD_LAYERS = (DEPTH + 1) // 2
N_MLSTM_LAYERS = DEPTH // 2
CONV_W = 4
SSD_INNER = 2 * D_MODEL
SSD_HEADDIM = 64
SSD_HEADS = SSD_INNER // SSD_HEADDIM
SSD_GROUPS = 8
SSD_HPG = SSD_HEADS // SSD_GROUPS
SSD_DSTATE = 128
SSD_CONV_DIM = SSD_INNER + 2 * SSD_GROUPS * SSD_DSTATE
SSD_PROJ = SSD_INNER + SSD_CONV_DIM + SSD_HEADS
ML_INNER = 2 * D_MODEL
ML_HEADS = 8
ML_HEADDIM = ML_INNER // ML_HEADS
ML_PROJ = 3 * ML_INNER + 2 * ML_HEADS
DEEPNORM_ALPHA = (2 * DEPTH) ** 0.25
DEEPNORM_BETA = (8 * DEPTH) ** -0.25
LN_EPS = 1e-5
RMS_EPS = 1e-6

kernel_name = "hybrid_ssd_mlstm_streaming_step"


def layer_norm(x, g, b):
    xf = x.astype(jnp.float32)
    mu = jnp.mean(xf, axis=-1, keepdims=True)
    var = jnp.mean(jnp.square(xf - mu), axis=-1, keepdims=True)
    return ((xf - mu) * lax.rsqrt(var + LN_EPS) * g + b).astype(x.dtype)


def causal_conv(u, conv_state, w, b):
    T = u.shape[1]
    up = jnp.concatenate([conv_state.astype(u.dtype), u], axis=1)
    y = b + up[:, 0:T] * w[0]
    for k in range(1, CONV_W):
        y = y + up[:, k:k + T] * w[k]
    return y, up[:, -(CONV_W - 1):]


def to_chunks(a, L):
    B, T = a.shape[0], a.shape[1]
    return jnp.moveaxis(a.reshape((B, T // L, L) + a.shape[2:]), 1, 0)


def from_chunks(a):
    a = jnp.moveaxis(a, 0, 1)
    return a.reshape((a.shape[0], a.shape[1] * a.shape[2]) + a.shape[3:])


def ssd_scan(x, dt, A, Bm, Cm, h0):
    T = x.shape[1]
    L = min(CHUNK, T)
    causal = jnp.tril(jnp.ones((L, L), dtype=bool))

    def step(h, inp):
        xc, dtc, Bc, Cc = inp
        cum = jnp.cumsum(dtc * A, axis=1)
        seg = cum[:, :, None] - cum[:, None, :]
        decay = jnp.exp(jnp.where(causal[None, :, :, None, None], seg, -jnp.inf))
        cb = jnp.einsum('btgn,bsgn->btsg', Cc, Bc)
        w = cb[..., None] * decay * dtc[:, None]
        y = jnp.einsum('btsgh,bsghp->btghp', w, xc)
        y = y + jnp.einsum('btgn,bghpn->btghp', Cc, h) * jnp.exp(cum)[..., None]
        tail = jnp.exp(cum[:, -1:] - cum) * dtc
        h = h * jnp.exp(cum[:, -1])[..., None, None] + jnp.einsum(
            'bsghp,bsgn->bghpn', xc * tail[..., None], Bc)
        return h, y

    hT, ys = lax.scan(step, h0, (to_chunks(x, L), to_chunks(dt, L), to_chunks(Bm, L), to_chunks(Cm, L)))
    return from_chunks(ys), hT


def mlstm_scan(q, k, v, li, lf, C0, n0, m0):
    T = q.shape[1]
    L = min(CHUNK, T)
    causal = jnp.tril(jnp.ones((L, L), dtype=bool))

    def step(carry, inp):
        C, n, m = carry
        qc, kc, vc, lic, lfc = inp
        b = jnp.cumsum(lfc, axis=1)
        dmat = b[:, :, None] - b[:, None, :] + lic[:, None]
        dmat = jnp.where(causal[None, :, :, None], dmat, -jnp.inf)
        inter = b + m[:, None]
        m_t = jnp.maximum(inter, jnp.max(dmat, axis=2))
        s = jnp.einsum('bthd,bshd->btsh', qc, kc) * jnp.exp(dmat - m_t[:, :, None])
        g = jnp.exp(inter - m_t)
        num = jnp.einsum('btsh,bshd->bthd', s, vc) + g[..., None] * jnp.einsum('bthk,bhkv->bthv', qc, C)
        den = jnp.sum(s, axis=2) + g * jnp.einsum('bthk,bhk->bth', qc, n)
        h = num / jnp.maximum(jnp.abs(den), jnp.exp(-m_t))[..., None]
        m_new = m_t[:, -1]
        tail = jnp.exp(b[:, -1:] - b + lic - m_new[:, None])
        dec = jnp.exp(b[:, -1] + m - m_new)
        kt = kc * tail[..., None]
        C = dec[..., None, None] * C + jnp.einsum('bshk,bshv->bhkv', kt, vc)
        n = dec[..., None] * n + jnp.sum(kt, axis=1)
        return (C, n, m_new), h

    (CT, nT, mT), hs = lax.scan(step, (C0, n0, m0),
                                (to_chunks(q, L), to_chunks(k, L), to_chunks(v, L),
                                 to_chunks(li, L), to_chunks(lf, L)))
    return from_chunks(hs), CT, nT, mT


def ssd_mixer(x, conv_state, h0, w_in, conv_w, conv_b, dt_bias, A_log, D, norm_w, w_out):
    Bsz, T = x.shape[0], x.shape[1]
    proj = x @ w_in
    z = proj[..., :SSD_INNER]
    xBC = proj[..., SSD_INNER:SSD_INNER + SSD_CONV_DIM]
    dt_raw = proj[..., SSD_INNER + SSD_CONV_DIM:]
    xBC, new_conv = causal_conv(xBC, conv_state, conv_w, conv_b)
    xBC = jax.nn.silu(xBC).astype(jnp.float32)
    xs = xBC[..., :SSD_INNER].reshape(Bsz, T, SSD_GROUPS, SSD_HPG, SSD_HEADDIM)
    Bm = xBC[..., SSD_INNER:SSD_INNER + SSD_GROUPS * SSD_DSTATE].reshape(Bsz, T, SSD_GROUPS, SSD_DSTATE)
    Cm = xBC[..., SSD_INNER + SSD_GROUPS * SSD_DSTATE:].reshape(Bsz, T, SSD_GROUPS, SSD_DSTATE)
    dt = jax.nn.softplus(dt_raw.astype(jnp.float32) + dt_bias).reshape(Bsz, T, SSD_GROUPS, SSD_HPG)
    A = -jnp.exp(A_log.astype(jnp.float32)).reshape(SSD_GROUPS, SSD_HPG)
    h0 = h0.astype(jnp.float32).reshape(Bsz, SSD_GROUPS, SSD_HPG, SSD_HEADDIM, SSD_DSTATE)
    y, hT = ssd_scan(xs, dt, A, Bm, Cm, h0)
    y = y + D.astype(jnp.float32).reshape(SSD_GROUPS, SSD_HPG)[:, :, None] * xs
    y = y.reshape(Bsz, T, SSD_GROUPS, -1) * jax.nn.silu(z.astype(jnp.float32)).reshape(Bsz, T, SSD_GROUPS, -1)
    y = y * lax.rsqrt(jnp.mean(jnp.square(y), axis=-1, keepdims=True) + RMS_EPS)
    y = y.reshape(Bsz, T, SSD_INNER) * norm_w
    out = y.astype(x.dtype) @ w_out
    return out, new_conv, hT.reshape(Bsz, SSD_HEADS, SSD_HEADDIM, SSD_DSTATE)


def mlstm_mixer(x, conv_state, C0, n0, m0, w_in, conv_w, conv_b, w_q, w_k, w_v, b_i, b_f, norm_w, skip, w_out):
    Bsz, T = x.shape[0], x.shape[1]
    proj = x @ w_in
    xm = proj[..., :ML_INNER]
    z = proj[..., ML_INNER:2 * ML_INNER]
    o_pre = proj[..., 2 * ML_INNER:3 * ML_INNER]
    gi = proj[..., 3 * ML_INNER:3 * ML_INNER + ML_HEADS]
    gf = proj[..., 3 * ML_INNER + ML_HEADS:]
    xc, new_conv = causal_conv(xm, conv_state, conv_w, conv_b)
    xc = jax.nn.silu(xc).astype(jnp.float32).reshape(Bsz, T, ML_HEADS, ML_HEADDIM)
    xmh = xm.astype(jnp.float32).reshape(Bsz, T, ML_HEADS, ML_HEADDIM)
    q = jnp.einsum('bthd,hde->bthe', xc, w_q) * (ML_HEADDIM ** -0.5)
    k = jnp.einsum('bthd,hde->bthe', xc, w_k)
    v = jnp.einsum('bthd,hde->bthe', xmh, w_v)
    li = gi.astype(jnp.float32) + b_i
    lf = jax.nn.log_sigmoid(gf.astype(jnp.float32) + b_f)
    h, CT, nT, mT = mlstm_scan(q, k, v, li, lf, C0.astype(jnp.float32),
                               n0.astype(jnp.float32), m0.astype(jnp.float32))
    mu = jnp.mean(h, axis=-1, keepdims=True)
    var = jnp.mean(jnp.square(h - mu), axis=-1, keepdims=True)
    hn = (h - mu) * lax.rsqrt(var + LN_EPS) * norm_w
    o = jax.nn.sigmoid(o_pre.astype(jnp.float32)).reshape(Bsz, T, ML_HEADS, ML_HEADDIM)
    hcell = (o * hn).reshape(Bsz, T, ML_INNER) + skip * xc.reshape(Bsz, T, ML_INNER)
    y = hcell * jax.nn.silu(z.astype(jnp.float32))
    out = y.astype(x.dtype) @ w_out
    return out, new_conv, CT, nT, mT


def trunk(x, states, weights):
    ssd_conv, ssd_h, ml_conv, ml_C, ml_n, ml_m = states
    (ssd_w_in, ssd_conv_w, ssd_conv_b, ssd_dt_bias, ssd_A_log, ssd_D, ssd_norm_w, ssd_w_out,
     ml_w_in, ml_conv_w, ml_conv_b, ml_w_q, ml_w_k, ml_w_v, ml_b_i, ml_b_f, ml_norm_w, ml_skip, ml_w_out,
     ln_g, ln_b) = weights
    o_sc, o_sh, o_mc, o_mC, o_mn, o_mm = [], [], [], [], [], []
    for i in range(DEPTH):
        j = i // N_MIXERS
        if i % N_MIXERS == 0:
            y, c, h = ssd_mixer(x, ssd_conv[j], ssd_h[j], ssd_w_in[j], ssd_conv_w[j], ssd_conv_b[j],
                                ssd_dt_bias[j], ssd_A_log[j], ssd_D[j], ssd_norm_w[j], ssd_w_out[j])
            o_sc.append(c)
            o_sh.append(h)
        else:
            y, c, C, n, m = mlstm_mixer(x, ml_conv[j], ml_C[j], ml_n[j], ml_m[j], ml_w_in[j], ml_conv_w[j],
                                        ml_conv_b[j], ml_w_q[j], ml_w_k[j], ml_w_v[j], ml_b_i[j], ml_b_f[j],
                                        ml_norm_w[j], ml_skip[j], ml_w_out[j])
            o_mc.append(c)
            o_mC.append(C)
            o_mn.append(n)
            o_mm.append(m)
        x = layer_norm(DEEPNORM_ALPHA * x + y, ln_g[i], ln_b[i])
    return (x, jnp.stack(o_sc), jnp.stack(o_sh), jnp.stack(o_mc), jnp.stack(o_mC),
            jnp.stack(o_mn), jnp.stack(o_mm))


def setup_inputs(seed: int = 0) -> dict:
    key = jax.random.key(seed)
    ks = iter(jax.random.split(key, 48))
    f32 = jnp.float32

    def nrm(shape, s):
        return jax.random.normal(next(ks), shape, f32) * s

    NA, NB = N_SSD_LAYERS, N_MLSTM_LAYERS
    dt0 = jnp.exp(jax.random.uniform(next(ks), (NA, SSD_HEADS), f32, math.log(1e-3), math.log(1e-1)))
    b_f = jnp.broadcast_to(jnp.linspace(3.0, 6.0, ML_HEADS, dtype=f32), (NB, ML_HEADS)) + nrm((NB, ML_HEADS), 0.1)
    return {
        "x_prompt": nrm((BATCH, SEQ, D_MODEL), 1.0),
        "x_sample": nrm((DEC_BATCH, DEC_SEQ, D_MODEL), 1.0),
        "state_ssd_conv": nrm((NA, DEC_BATCH, CONV_W - 1, SSD_CONV_DIM), 1.0),
        "state_ssd_h": nrm((NA, DEC_BATCH, SSD_HEADS, SSD_HEADDIM, SSD_DSTATE), 0.1),
        "state_mlstm_conv": nrm((NB, DEC_BATCH, CONV_W - 1, ML_INNER), 1.0),
        "state_mlstm_C": nrm((NB, DEC_BATCH, ML_HEADS, ML_HEADDIM, ML_HEADDIM), 0.1),
        "state_mlstm_n": nrm((NB, DEC_BATCH, ML_HEADS, ML_HEADDIM), 0.1),
        "state_mlstm_m": nrm((NB, DEC_BATCH, ML_HEADS), 1.0),
        "ssd_w_in": nrm((NA, D_MODEL, SSD_PROJ), D_MODEL ** -0.5),
        "ssd_conv_w": nrm((NA, CONV_W, SSD_CONV_DIM), CONV_W ** -0.5),
        "ssd_conv_b": nrm((NA, SSD_CONV_DIM), 0.01),
        "ssd_dt_bias": dt0 + jnp.log(-jnp.expm1(-dt0)),
        "ssd_A_log": jnp.log(jax.random.uniform(next(ks), (NA, SSD_HEADS), f32, 1.0, 16.0)),
        "ssd_D": 1.0 + nrm((NA, SSD_HEADS), 0.1),
        "ssd_norm_w": 1.0 + nrm((NA, SSD_INNER), 0.02),
        "ssd_w_out": nrm((NA, SSD_INNER, D_MODEL), SSD_INNER ** -0.5 * DEEPNORM_BETA),
        "ml_w_in": nrm((NB, D_MODEL, ML_PROJ), D_MODEL ** -0.5),
        "ml_conv_w": nrm((NB, CONV_W, ML_INNER), CONV_W ** -0.5),
        "ml_conv_b": nrm((NB, ML_INNER), 0.01),
        "ml_w_q": nrm((NB, ML_HEADS, ML_HEADDIM, ML_HEADDIM), ML_HEADDIM ** -0.5),
        "ml_w_k": nrm((NB, ML_HEADS, ML_HEADDIM, ML_HEADDIM), ML_HEADDIM ** -0.5),
        "ml_w_v": nrm((NB, ML_HEADS, ML_HEADDIM, ML_HEADDIM), ML_HEADDIM ** -0.5),
        "ml_b_i": nrm((NB, ML_HEADS), 0.1),
        "ml_b_f": b_f,
        "ml_norm_w": 1.0 + nrm((NB, ML_HEADS, ML_HEADDIM), 0.02),
        "ml_skip": 1.0 + nrm((NB, ML_INNER), 0.1),
        "ml_w_out": nrm((NB, ML_INNER, D_MODEL), ML_INNER ** -0.5 * DEEPNORM_BETA),
        "ln_g": 1.0 + nrm((DEPTH, D_MODEL), 0.01),
        "ln_b": nrm((DEPTH, D_MODEL), 0.01),
    }


def reference(x_prompt, x_sample, state_ssd_conv, state_ssd_h, state_mlstm_conv, state_mlstm_C,
              state_mlstm_n, state_mlstm_m,
              ssd_w_in, ssd_conv_w, ssd_conv_b, ssd_dt_bias, ssd_A_log, ssd_D, ssd_norm_w, ssd_w_out,
              ml_w_in, ml_conv_w, ml_conv_b, ml_w_q, ml_w_k, ml_w_v, ml_b_i, ml_b_f, ml_norm_w, ml_skip,
              ml_w_out, ln_g, ln_b):
    weights = (ssd_w_in, ssd_conv_w, ssd_conv_b, ssd_dt_bias, ssd_A_log, ssd_D, ssd_norm_w, ssd_w_out,
               ml_w_in, ml_conv_w, ml_conv_b, ml_w_q, ml_w_k, ml_w_v, ml_b_i, ml_b_f, ml_norm_w, ml_skip,
               ml_w_out, ln_g, ln_b)
    f32 = jnp.float32
    NA, NB = N_SSD_LAYERS, N_MLSTM_LAYERS
    zero_states = (jnp.zeros((NA, BATCH, CONV_W - 1, SSD_CONV_DIM), x_prompt.dtype),
                   jnp.zeros((NA, BATCH, SSD_HEADS, SSD_HEADDIM, SSD_DSTATE), f32),
                   jnp.zeros((NB, BATCH, CONV_W - 1, ML_INNER), x_prompt.dtype),
                   jnp.zeros((NB, BATCH, ML_HEADS, ML_HEADDIM, ML_HEADDIM), f32),
                   jnp.zeros((NB, BATCH, ML_HEADS, ML_HEADDIM), f32),
                   jnp.zeros((NB, BATCH, ML_HEADS), f32))
    y_prompt, p_sc, p_sh, p_mc, p_mC, p_mn, p_mm = trunk(x_prompt, zero_states, weights)
    cache_states = (state_ssd_conv, state_ssd_h, state_mlstm_conv, state_mlstm_C, state_mlstm_n, state_mlstm_m)
    y_sample, s_sc, s_sh, s_mc, s_mC, s_mn, s_mm = trunk(x_sample, cache_states, weights)
    return (y_prompt, y_sample, p_sc, p_sh, p_mc, p_mC, p_mn, p_mm, s_sc, s_sh, s_mc, s_mC, s_mn, s_mm)
```

```python
import os
import numpy as np
import concourse.bass as bass
import concourse.mybir as mybir
from concourse.bass_utils import run_bass_kernel_spmd
from contextlib import ExitStack

F32 = mybir.dt.float32
BF16 = mybir.dt.bfloat16
AF = mybir.ActivationFunctionType
ALU = mybir.AluOpType

D = 2048
KT = 16
DEPTH = 4
ALPHA = float((2 * DEPTH) ** 0.25)
LN_EPS = 1e-5
RMS_EPS = 1e-6
S_PROJ = 10304
M_PROJ = 12304
SAME_ENG_SYNC = os.environ.get("KSAME", "1") == "1"
import os
NCORES = int(os.environ.get('KCORES', '8'))


class Buf:
    __slots__ = ("name", "lw", "rd", "excl")

    def __init__(self, name, excl=False):
        self.name = name
        self.lw = []
        self.rd = []
        self.excl = excl


class Op:
    __slots__ = ("eng", "fn", "deps", "dma", "sig", "cnt", "semi")

    def __init__(self, eng, fn, deps, dma):
        self.eng = eng
        self.fn = fn
        self.deps = deps
        self.dma = dma
        self.sig = False
        self.cnt = 0
        self.semi = 0


class Sched:
    ENGS = ["pe", "act", "dve", "pool", "sp"]
    RING = 12

    def __init__(self):
        self.ops = []
        self.eng_ops = {e: [] for e in self.ENGS}
        self.last = {e: None for e in self.ENGS}

    MAXOPS = int(os.environ.get("KMAXOPS", "100000000"))

    def add(self, eng, fn, r=(), w=(), dma=False, conc=False, extra=()):
        i = len(self.ops)
        if i >= self.MAXOPS:
            return None
        deps = set(extra)
        if any(b.excl for b in r):
            w = list(w) + [b for b in r if b.excl and b not in w]
            r = [b for b in r if not b.excl]
        for b in r:
            deps.update(b.lw)
        for b in w:
            if not conc:
                deps.update(b.lw)
            deps.update(b.rd)
        for b in r:
            b.rd.append(i)
        for b in w:
            if conc:
                b.lw.append(i)
            else:
                b.lw = [i]
            b.rd = []
        deps.discard(i)
        op = Op(eng, fn, deps, dma)
        self.ops.append(op)
        self.eng_ops[eng].append(i)
        self.last[eng] = i
        return i

    def fence(self, dummies):
        lasts = [self.last[e] for e in ("pe", "act", "dve", "pool") if self.last[e] is not None]
        for e in ("act", "dve", "pool"):
            t = dummies[e]
            if e == "act":
                z = dummies["zsrc"]
                self.add(e, (lambda t, z: (lambda en: en.activation(out=t, in_=z, func=AF.Copy)))(t, z),
                         r=[dummies["zbuf"]], extra=lasts)
            else:
                self.add(e, (lambda t: (lambda en: en.memset(t, 0.0)))(t), extra=lasts)

    def emit(self, nc, es):
        ops = self.ops
        comp = ("act", "dve", "pool")
        for op in ops:
            for d in op.deps:
                t = ops[d]
                if t.dma:
                    continue
                if t.eng != op.eng or (SAME_ENG_SYNC and t.eng in comp):
                    t.sig = True
        ndma = {}
        for e in self.ENGS:
            c = 0
            k = 0
            for i in self.eng_ops[e]:
                op = ops[i]
                if op.dma:
                    op.semi = k % self.RING
                    op.cnt = 16 * (k // self.RING + 1)
                    k += 1
                else:
                    if op.sig:
                        c += 1
                    op.cnt = c
            ndma[e] = k
        sem_e = {e: es.enter_context(nc.semaphore("sem_" + e)) for e in ("pe", "act", "dve", "pool")}
        sem_d = {}
        for e in self.ENGS:
            if ndma[e] > 0:
                sem_d[e] = [es.enter_context(nc.semaphore("semd_%s_%d" % (e, j)))
                            for j in range(min(self.RING, ndma[e]))]
        block = es.enter_context(nc.Block())

        def run(e, eng):
            waited = {}
            for i in self.eng_ops[e]:
                op = ops[i]
                need = {}
                for d in op.deps:
                    t = ops[d]
                    if t.dma:
                        key = ("d", t.eng, t.semi)
                    else:
                        if t.eng == e and not (SAME_ENG_SYNC and e in comp):
                            continue
                        key = ("e", t.eng)
                    if t.cnt > need.get(key, 0):
                        need[key] = t.cnt
                if op.dma and op.cnt > 16:
                    key = ("d", e, op.semi)
                    need[key] = max(need.get(key, 0), op.cnt - 16)
                for key, val in need.items():
                    if waited.get(key, 0) < val:
                        sem = sem_e[key[1]] if key[0] == "e" else sem_d[key[1]][key[2]]
                        eng.wait_ge(sem, val)
                        waited[key] = val
                ins = op.fn(eng)
                if op.dma:
                    ins.then_inc(sem_d[e][op.semi], 16)
                elif op.sig:
                    ins.then_inc(sem_e[e], 1)
            if ndma[e] > 0:
                k = ndma[e]
                for j in range(min(self.RING, k)):
                    n_on = (k - 1 - j) // self.RING + 1
                    if waited.get(("d", e, j), 0) < 16 * n_on:
                        eng.wait_ge(sem_d[e][j], 16 * n_on)

        @block.tensor
        def _(eng):
            run("pe", eng)

        @block.scalar
        def _(eng):
            run("act", eng)

        @block.vector
        def _(eng):
            run("dve", eng)

        @block.gpsimd
        def _(eng):
            run("pool", eng)

        @block.sync
        def _(eng):
            run("sp", eng)


class Ring:
    def __init__(self, items):
        self.items = items
        self.i = 0

    def next(self):
        it = self.items[self.i % len(self.items)]
        self.i += 1
        return it


def build(NPB, NS, NT):
    nc = bass.Bass("TRN2", target_bir_lowering=False)
    S = Sched()
    es = ExitStack()
    TP = NPB * NT
    NTT = NT // 128

    def din(name, shape):
        return nc.dram_tensor(name, list(shape), F32, kind="ExternalInput").ap()

    def dout(name, shape):
        return nc.dram_tensor(name, list(shape), F32, kind="ExternalOutput").ap()

    def dscr(name, shape, dt):
        return nc.dram_tensor(name, list(shape), dt, kind="Internal").ap()

    xp = din("xp", [TP, D])
    xs = din("xs", [NS, 16, D])
    i_sc = din("i_sc", [2, NS, 3, 6144])
    i_sh = din("i_sh", [2, NS, 64, 64, 128])
    i_mc = din("i_mc", [2, NS, 3, 4096])
    i_mC = din("i_mC", [2, NS, 8, 512, 512])
    i_mn = din("i_mn", [2, NS, 8, 512])
    i_mm = din("i_mm", [2, NS, 8])
    w_sin = din("ssd_w_in", [2, D, S_PROJ])
    w_scw = din("ssd_conv_w", [2, 4, 6144])
    w_scb = din("ssd_conv_b", [2, 6144])
    w_sdtb = din("ssd_dt_bias", [2, 64])
    w_sA = din("ssd_A_log", [2, 64])
    w_sD = din("ssd_D", [2, 64])
    w_snw = din("ssd_norm_w", [2, 4096])
    w_sout = din("ssd_w_out", [2, 4096, D])
    w_min = din("ml_w_in", [2, D, M_PROJ])
    w_mcw = din("ml_conv_w", [2, 4, 4096])
    w_mcb = din("ml_conv_b", [2, 4096])
    w_mq = din("ml_w_q", [2, 8, 512, 512])
    w_mk = din("ml_w_k", [2, 8, 512, 512])
    w_mv = din("ml_w_v", [2, 8, 512, 512])
    w_mbi = din("ml_b_i", [2, 8])
    w_mbf = din("ml_b_f", [2, 8])
    w_mnw = din("ml_norm_w", [2, 8, 512])
    w_msk = din("ml_skip", [2, 4096])
    w_mout = din("ml_w_out", [2, 4096, D])
    w_lng = din("ln_g", [4, D])
    w_lnb = din("ln_b", [4, D])

    yp = dout("yp", [TP, D])
    ys = dout("ys", [NS, 16, D])
    o_sc = [dout("p_sc", [2, 3, 6144]), dout("s_sc", [2, NS, 3, 6144])]
    o_sh = [dout("p_sh", [2, 64, 64, 128]), dout("s_sh", [2, NS, 64, 64, 128])]
    o_mc = [dout("p_mc", [2, 3, 4096]), dout("s_mc", [2, NS, 3, 4096])]
    o_mC = [dout("p_mC", [2, 8, 512, 512]), dout("s_mC", [2, NS, 8, 512, 512])]
    o_mn = [dout("p_mn", [2, 8, 512]), dout("s_mn", [2, NS, 8, 512])]
    o_mm = [dout("p_mm", [2, 8]), dout("s_mm", [2, NS, 8])]

    c_sin = dscr("c_sin", [2, 16, 128, 16 * 512], BF16)
    c_sbc = dscr("c_sbc", [2, 8, 128, 16 * 256], BF16)
    c_sdt = dscr("c_sdt", [2, 128, 16 * 64], BF16)
    c_sout = dscr("c_sout", [2, 8, 128, 16 * 512], BF16)
    c_min = dscr("c_min", [2, 24, 128, 16 * 512], BF16)
    c_mg = dscr("c_mg", [2, 128, 16 * 16], BF16)
    c_mout = dscr("c_mout", [2, 8, 128, 16 * 512], BF16)
    c_mqkv = dscr("c_mqkv", [2, 8, 128, 12 * 512], BF16)
    c_scr = dscr("c_scr", [2, 8, 512, 512], F32)
    WB = {}

    def cvt(name, dst, src, C):
        if name not in WB:
            WB[name] = Buf("wb_" + name)
        o = dst.rearrange("p (k c) -> p k c", c=C)
        i_ = src.rearrange("(k p) c -> p k c", p=128)
        S.add("pool", (lambda o, i_: (lambda e: e.dma_start(out=o, in_=i_)))(o, i_), w=[WB[name]], dma=True, conc=True)

    def sb(name, shape, dt):
        return es.enter_context(nc.sbuf_tensor(name, list(shape), dt))

    ident_bf = sb("ident_bf", [128, 128], BF16)
    ident_f = sb("ident_f", [128, 128], F32)
    mask01 = sb("mask01", [128, 128], F32)
    Umat = sb("Umat", [128, 128], F32)
    ones_f = sb("ones_f", [128, 128], F32)
    zeros_f = sb("zeros_f", [128, 128], F32)
    CB = Buf("consts")

    p_scw = sb("p_scw", [128, 2, 4, 48], F32)
    p_scb = sb("p_scb", [128, 2, 48], F32)
    p_snw = sb("p_snw", [128, 2, 32], F32)
    p_sdtb = sb("p_sdtb", [128, 2, 64], F32)
    p_sA = sb("p_sA", [128, 2, 64], F32)
    p_sD = sb("p_sD", [128, 2, 64], F32)
    p_mcw = sb("p_mcw", [128, 2, 4, 32], F32)
    p_mcb = sb("p_mcb", [128, 2, 32], F32)
    p_mnw = sb("p_mnw", [128, 2, 32], F32)
    p_msk = sb("p_msk", [128, 2, 32], F32)
    p_mbi = sb("p_mbi", [8, 2], F32)
    p_mnbf = sb("p_mnbf", [8, 2], F32)
    PB = Buf("params")

    xres = sb("xres", [128, NTT, D], F32)
    xres_b = [Buf("xres%d" % t) for t in range(NTT)]
    xT = sb("xT", [128, KT, NT], BF16)
    xT_b = [Buf("xT%d" % t) for t in range(NTT)]
    yT = sb("yT", [128, 32, NT], BF16)
    yT_b = [Buf("yT%d" % g) for g in range(8)]
    NW = 3
    wbufs = Ring([(sb("wbuf%d" % i, [128, 16, 512], BF16), Buf("wbuf%d" % i)) for i in range(NW)])
    hst = sb("hst", [128, 2, 4096], F32)
    hst_b = [[Buf("hst%d_%d" % (j, g)) for g in range(8)] for j in range(2)]
    hbf2 = [sb("hbf%d" % i, [128, 512], BF16) for i in range(2)]
    hbf2_b = [Buf("hbf%d" % i) for i in range(2)]
    carry_s = sb("carry_s", [128, 2, 48, 3], F32)
    carry_s_b = [[Buf("cs%d_%d" % (j, g)) for g in range(10)] for j in range(2)]
    carry_m = sb("carry_m", [128, 2, 32, 3], F32)
    carry_m_b = [[Buf("cm%d_%d" % (j, h)) for h in range(8)] for j in range(2)]
    nst = sb("nst", [128, 2, 8, 4], F32)
    nst_b = [[Buf("nst%d_%d" % (j, h)) for h in range(8)] for j in range(2)]
    nbf = sb("nbf", [128, 4], BF16)
    nbf_b = Buf("nbf")
    mst = sb("mst", [8, 2], F32)
    mst_b = [Buf("mst0"), Buf("mst1")]
    Cst = Ring([(sb("Cst%d" % i, [128, 4, 512], F32), Buf("Cst%d" % i)) for i in range(2)])
    Cbf = sb("Cbf", [128, 4, 512], BF16)
    Cbf_b = Buf("Cbf")
    stg = Ring([(sb("stg%d" % i, [128, 512], F32), Buf("stg%d" % i)) for i in range(2)])
    dummies = {e: sb("dummy_" + e, [1, 4], F32) for e in ("act", "dve", "pool")}
    dummies = {e: t[:] for e, t in dummies.items()}
    dummies["zsrc"] = zeros_f[0:1, 0:4]
    dummies["zbuf"] = CB

    ARENA_B = 46 * 1024
    arena = sb("arena", [128, ARENA_B // 2], BF16)
    apos = [0]

    def areset():
        apos[0] = 0

    def aalloc(name, free_shape, dt, parts=128):
        n = 1
        for s_ in free_shape:
            n *= s_
        nb = n * (4 if dt == F32 else 2)
        nb = (nb + 3) // 4 * 4
        o = apos[0]
        apos[0] += nb
        assert apos[0] <= ARENA_B, ("arena overflow", name, apos[0])
        v = arena[0:parts, o // 2:(o + nb) // 2]
        if dt == F32:
            v = v.bitcast(F32)
        n_el = nb // (4 if dt == F32 else 2)
        v = v[:, 0:n]
        if len(free_shape) == 2:
            v = v.rearrange("p (a b) -> p a b", b=free_shape[1])
        elif len(free_shape) == 3:
            v = v.rearrange("p (a b c) -> p a b c", b=free_shape[1], c=free_shape[2])
        return v, Buf(name)

    psA = Ring([(es.enter_context(nc.psum_tensor("psA%d" % i, [128, 512], F32)), Buf("psA%d" % i, excl=True)) for i in range(6)])
    psB = Ring([(es.enter_context(nc.psum_tensor("psB%d" % i, [128, 1024], BF16)), Buf("psB%d" % i, excl=True)) for i in range(2)])

    def MM(out, lhsT, rhs, start, stop, r, w, skip=False):
        if skip:
            return S.add("pe", lambda e: e.matmul(out, lhsT=lhsT, rhs=rhs, start=start, stop=stop,
                                                  skip_group_check=True), r, w)
        return S.add("pe", lambda e: e.matmul(out, lhsT=lhsT, rhs=rhs, start=start, stop=stop), r, w)

    def TR(out, in_, ident, r, w):
        return S.add("pe", lambda e: e.transpose(out, in_, ident), r, w)

    def ACT(out, in_, func, r, w, bias=None, scale=None, accum=None):
        kw = {}
        if bias is not None:
            kw["bias"] = bias
        if scale is not None:
            kw["scale"] = scale
        if accum is not None:
            kw["accum_out"] = accum
        return S.add("act", lambda e: e.activation(out=out, in_=in_, func=func, **kw), r, w)

    def TT(eng, out, in0, in1, op, r, w):
        return S.add(eng, lambda e: e.tensor_tensor(out=out, in0=in0, in1=in1, op=op), r, w)

    def TS(eng, out, in0, s1, s2, op0, op1, r, w):
        if op1 is None:
            return S.add(eng, lambda e: e.tensor_scalar(out=out, in0=in0, scalar1=s1, scalar2=None, op0=op0), r, w)
        return S.add(eng, lambda e: e.tensor_scalar(out=out, in0=in0, scalar1=s1, scalar2=s2, op0=op0, op1=op1), r, w)

    def STT(out, in0, scalar, in1, op0, op1, r, w):
        return S.add("dve", lambda e: e.scalar_tensor_tensor(out=out, in0=in0, scalar=scalar, in1=in1,
                                                             op0=op0, op1=op1), r, w)

    def CP(eng, out, in_, r, w):
        if eng == "act":
            return S.add("act", lambda e: e.activation(out=out, in_=in_, func=AF.Copy), r, w)
        return S.add(eng, lambda e: e.tensor_copy(out=out, in_=in_), r, w)

    def MS(eng, ap, val, w):
        return S.add(eng, lambda e: e.memset(ap, val), (), w)

    def DMA(q, out, in_, r, w, slow=False, conc=False):
        if slow:
            return S.add(q, lambda e: e.dma_start(out=out, in_=in_, allow_slow_non_contiguous=True), r, w,
                         dma=True, conc=conc)
        return S.add(q, lambda e: e.dma_start(out=out, in_=in_), r, w, dma=True, conc=conc)

    def mark(name):
        if os.environ.get("KPRINT", "0") == "1":
            print("MARK", name, len(S.ops), flush=True)

    MS("pool", ones_f[:], 1.0, [CB])
    MS("pool", zeros_f[:], 0.0, [CB])
    S.add("pool", lambda e: e.affine_select(out=mask01[:], in_=ones_f[:], pattern=[[1, 128]], compare_op=ALU.is_ge,
                                            fill=0.0, base=0, channel_multiplier=-1), [CB], [CB])
    S.add("pool", lambda e: e.affine_select(out=Umat[:], in_=ones_f[:], pattern=[[-1, 128]], compare_op=ALU.is_gt,
                                            fill=0.0, base=0, channel_multiplier=1), [CB], [CB])
    S.add("pool", lambda e: e.affine_select(out=ident_f[:], in_=ones_f[:], pattern=[[-1, 128]],
                                            compare_op=ALU.is_equal, fill=0.0, base=0, channel_multiplier=1),
          [CB], [CB])
    CP("pool", ident_bf[:], ident_f[:], [CB], [CB])

    for j in range(2):
        for k in range(4):
            DMA("sp", p_scw[:, j, k, :], w_scw[j, k].rearrange("(c p) -> p c", p=128), [], [PB], slow=True, conc=True)
            DMA("sp", p_mcw[:, j, k, :], w_mcw[j, k].rearrange("(c p) -> p c", p=128), [], [PB], slow=True, conc=True)
        DMA("sp", p_scb[:, j, :], w_scb[j].rearrange("(c p) -> p c", p=128), [], [PB], slow=True, conc=True)
        DMA("sp", p_snw[:, j, :], w_snw[j].rearrange("(c p) -> p c", p=128), [], [PB], slow=True, conc=True)
        DMA("sp", p_mcb[:, j, :], w_mcb[j].rearrange("(c p) -> p c", p=128), [], [PB], slow=True, conc=True)
        DMA("sp", p_mnw[:, j, :], w_mnw[j].rearrange("h e -> (h e)").rearrange("(c p) -> p c", p=128), [], [PB], slow=True, conc=True)
        DMA("sp", p_msk[:, j, :], w_msk[j].rearrange("(c p) -> p c", p=128), [], [PB], slow=True, conc=True)
        DMA("sp", p_sdtb[:, j, :], w_sdtb[j].partition_broadcast(128), [], [PB], conc=True)
        DMA("sp", p_sA[:, j, :], w_sA[j].partition_broadcast(128), [], [PB], conc=True)
        DMA("sp", p_sD[:, j, :], w_sD[j].partition_broadcast(128), [], [PB], conc=True)
        DMA("sp", p_mbi[:, j:j + 1], w_mbi[j].rearrange("(p o) -> p o", o=1), [], [PB], slow=True, conc=True)
        DMA("sp", p_mnbf[:, j:j + 1], w_mbf[j].rearrange("(p o) -> p o", o=1), [], [PB], slow=True, conc=True)
    ACT(p_sA[:], p_sA[:], AF.Exp, [PB], [PB])
    TS("dve", p_sA[:], p_sA[:], -1.0, None, ALU.mult, None, [PB], [PB])
    TS("dve", p_mnbf[:], p_mnbf[:], -1.0, None, ALU.mult, None, [PB], [PB])

    for j in range(2):
        cvt("sdt%d" % j, c_sdt[j], w_sin[j][:, 10240:10304], 64)
        for g in range(8):
            cvt("sin%d" % j, c_sin[j, g], w_sin[j][:, g * 512:(g + 1) * 512], 512)
            cvt("sin%d" % j, c_sin[j, 8 + g], w_sin[j][:, 4096 + g * 512:4096 + (g + 1) * 512], 512)
        for g in range(8):
            dstv = c_sbc[j, g].rearrange("p (k c) -> p k c", c=256)
            for bc in range(2):
                col0 = 8192 + bc * 1024 + g * 128
                S.add("pool", (lambda o, i_: (lambda e: e.dma_start(out=o, in_=i_)))(
                    dstv[:, :, bc * 128:(bc + 1) * 128],
                    w_sin[j][:, col0:col0 + 128].rearrange("(k p) c -> p k c", p=128)),
                    w=[WB["sin%d" % j]], dma=True, conc=True)
        for dc in range(4):
            for kh in range(2):
                cvt("sout%d" % j, c_sout[j, dc * 2 + kh], w_sout[j][kh * 2048:(kh + 1) * 2048, dc * 512:(dc + 1) * 512], 512)
        cvt("mg%d" % j, c_mg[j], w_min[j][:, 12288:12304], 16)
        for h in range(8):
            for which in range(3):
                cvt("min%d" % j, c_min[j, which * 8 + h],
                    w_min[j][:, which * 4096 + h * 512:which * 4096 + (h + 1) * 512], 512)
            for qi, wsrc in enumerate((w_mq, w_mk, w_mv)):
                cvt("mqkv%d" % j, c_mqkv[j, h][:, qi * 2048:(qi + 1) * 2048], wsrc[j, h], 512)
        for dc in range(4):
            for kh in range(2):
                cvt("mout%d" % j, c_mout[j, dc * 2 + kh], w_mout[j][kh * 2048:(kh + 1) * 2048, dc * 512:(dc + 1) * 512], 512)

    def wload(pieces, wbname):
        wt, wbf = wbufs.next()
        for (dstf, src) in pieces:
            DMA("sp", dstf(wt), src, [WB[wbname]], [wbf], conc=True)
        return wt, wbf

    def wfull(chunk_ap, wbname, K=16, C=512):
        return wload([(lambda wt: wt[:, 0:K, 0:C], chunk_ap.rearrange("p (k c) -> p k c", c=C))], wbname)

    def make_xT(ntok, TL, ntt):
        for t in range(ntt):
            xb, xb_b = aalloc_xb
            CP("dve", xb[0:TL, 0:1024], xres[0:TL, t, 0:1024], [xres_b[t]], [xb_b])
            CP("act", xb[0:TL, 1024:2048], xres[0:TL, t, 1024:2048], [xres_b[t]], [xb_b])
            for q in range(4):
                ps, psb = psB.next()
                for kk in range(4):
                    kt = q * 4 + kk
                    TR(ps[:, kk * 128:kk * 128 + TL], xb[0:TL, kt * 128:(kt + 1) * 128], ident_bf[0:TL, 0:TL],
                       [xb_b, CB], [psb])
                src = ps[:, 0:512].rearrange("p (k t) -> p k t", t=128)[:, :, 0:TL]
                CP("act" if q % 2 == 0 else "dve", xT[:, q * 4:(q + 1) * 4, t * 128:t * 128 + TL], src,
                   [psb], [xT_b[t]])

    def out_proj_ln(i, bw, wbname, ntok, TL, ntt, last, ydst):
        mark("outproj")
        for dc in range(4):
            pss = [psA.next() for _ in range(ntt)]
            for kh in range(2):
                wt, wbf = wfull(bw[dc * 2 + kh], wbname)
                for t in range(ntt):
                    ps, psb = pss[t]
                    for k in range(16):
                        MM(ps[0:TL, :], yT[:, kh * 16 + k, t * 128:t * 128 + TL], wt[:, k, :],
                           (kh == 0 and k == 0), (kh == 1 and k == 15), [wbf] + yT_b, [psb])
            for t in range(ntt):
                ps, psb = pss[t]
                xs_ = xres[0:TL, t, dc * 512:(dc + 1) * 512]
                STT(xs_, xs_, ALPHA, ps[0:TL, :], ALU.mult, ALU.add, [psb, xres_b[t]], [xres_b[t]])
        mark("ln")
        lw, lwb = wbufs.next()
        lnp = lw[:].rearrange("p a b -> p (a b)").bitcast(F32).rearrange("p (a b) -> p a b", b=D)
        DMA("sp", lnp[:, 0, :], w_lng[i].partition_broadcast(128), [], [lwb], conc=True)
        DMA("sp", lnp[:, 1, :], w_lnb[i].partition_broadcast(128), [], [lwb], conc=True)
        for t in range(ntt):
            st, st_b = aalloc_st
            for q in range(4):
                S.add("dve", (lambda o, i_: (lambda e: e.bn_stats(out=o, in_=i_)))(
                    st[0:TL, q, :], xres[0:TL, t, q * 512:(q + 1) * 512]), [xres_b[t]], [st_b])
            mv, mv_b = aalloc_mv
            S.add("dve", (lambda o, i_: (lambda e: e.bn_aggr(out=o, in_=i_)))(
                mv[0:TL, 0:2], st[0:TL, :, :].rearrange("p a b -> p (a b)")), [st_b], [mv_b])
            TS("dve", mv[0:TL, 2:3], mv[0:TL, 1:2], LN_EPS, None, ALU.add, None, [mv_b], [mv_b])
            ACT(mv[0:TL, 2:3], mv[0:TL, 2:3], AF.Sqrt, [mv_b], [mv_b])
            S.add("dve", (lambda o, i_: (lambda e: e.reciprocal(out=o, in_=i_)))(mv[0:TL, 2:3], mv[0:TL, 2:3]),
                  [mv_b], [mv_b])
            xr = xres[0:TL, t, :]
            TS("dve", xr, xr, mv[0:TL, 0:1], mv[0:TL, 2:3], ALU.subtract, ALU.mult, [mv_b, xres_b[t]], [xres_b[t]])
            TT("dve", xr, xr, lnp[0:TL, 0, :], ALU.mult, [lwb, xres_b[t]], [xres_b[t]])
            TT("dve", xr, xr, lnp[0:TL, 1, :], ALU.add, [lwb, xres_b[t]], [xres_b[t]])
            if last:
                DMA("sp", ydst[t * 128:t * 128 + TL, :], xr, [xres_b[t]], [])

    xb_t = sb("xb_t", [128, D], BF16)
    aalloc_xb = (xb_t, Buf("xb_t"))
    st_t = sb("st_t", [128, 4, 6], F32)
    aalloc_st = (st_t, Buf("st_t"))
    mv_t = sb("mv_t", [128, 4], F32)
    aalloc_mv = (mv_t, Buf("mv_t"))

    def conv_silu(raw, raw_b, ntok, cw, cb_, out, out_b, tmp, tmp_b):
        TS("dve", tmp[:, 0:ntok], raw[:, 0:ntok], cw(0), cb_, ALU.mult, ALU.add, [raw_b, PB], [tmp_b])
        for k in range(1, 4):
            STT(tmp[:, 0:ntok], raw[:, k:k + ntok], cw(k), tmp[:, 0:ntok], ALU.mult, ALU.add,
                [raw_b, PB, tmp_b], [tmp_b])
        ACT(out, tmp[:, 0:ntok], AF.Silu, [tmp_b], [out_b])

    def ssd_layer(j, ntok, L, first, lastblk, seq):
        nch = ntok // L
        TL = L
        areset()
        S.fence(dummies)
        wn = "sin%d" % j
        dtv, dtv_b = aalloc("dtv", [nch, 64], F32)
        av, av_b = aalloc("av", [nch, 64], F32)
        cumv, cum_b = aalloc("cumv", [nch, 64], F32)
        ev, ev_b = aalloc("ev", [nch, 64], F32)
        tlv, tl_b = aalloc("tlv", [nch, 64], F32)
        decv, dec_b = aalloc("decv", [nch, 64], F32)
        zs, zs_b = aalloc("zs", [nch, 512], BF16)
        raw, raw_b = aalloc("raw", [6, 3 + ntok], F32)
        ctmp, ctmp_b = aalloc("ctmp", [ntok], F32)
        xc, xc_b = aalloc("xc", [6, ntok], BF16)
        xtok, xtok_b = aalloc("xtok", [512], BF16)
        btok, btok_b = aalloc("btok", [128], BF16)
        xdt, xdt_b = aalloc("xdt", [512], BF16)
        xD, xD_b = aalloc("xD", [512], BF16)
        xtl, xtl_b = aalloc("xtl", [512], BF16)
        cbm, cbm_b = aalloc("cbm", [128], F32)
        aV, aV_b = aalloc("aV", [8, 128], F32)
        eM, eM_b = aalloc("eM", [8, 128], F32)
        WT, WT_b = aalloc("WT", [8, 128], BF16)
        y1, y1_b = aalloc("y1", [512], F32)
        y2, y2_b = aalloc("y2", [512], F32)
        yn, yn_b = aalloc("yn", [512], BF16)
        ss, ss_b = aalloc("ss", [4], F32)
        htmp, htmp_b = aalloc("htmp", [512], F32)

        if first:
            if seq["kind"] == "p":
                for g in range(8):
                    MS("pool", hst[:, j, g * 512:(g + 1) * 512], 0.0, [hst_b[j][g]])
                for g in range(10):
                    lo, hi = (g * 4, g * 4 + 4) if g < 8 else (32 + (g - 8) * 8, 40 + (g - 8) * 8)
                    MS("pool", carry_s[:, j, lo:hi, :], 0.0, [carry_s_b[j][g]])
            else:
                b = seq["b"]
                src = i_sh[j, b].rearrange("h p n -> (h p) n")
                for q in range(8):
                    sg, sg_b = stg.next()
                    DMA("sp", sg[:, :].rearrange("p (a n) -> p a n", n=128),
                        src[q * 512:(q + 1) * 512, :].rearrange("(a p) n -> p a n", p=128), [], [sg_b])
                    ps, psb = psA.next()
                    for a in range(4):
                        TR(ps[:, a * 128:(a + 1) * 128], sg[:, a * 128:(a + 1) * 128], ident_f[:], [sg_b, CB], [psb])
                    CP("act", hst[:, j, q * 512:(q + 1) * 512], ps[:, :], [psb], [hst_b[j][q]])
                for q in range(12):
                    sg, sg_b = stg.next()
                    DMA("sp", sg[0:3, :], i_sc[j, b][:, q * 512:(q + 1) * 512], [], [sg_b])
                    ps, psb = psA.next()
                    for a in range(4):
                        TR(ps[:, a * 3:a * 3 + 3], sg[0:3, a * 128:(a + 1) * 128], ident_f[0:3, 0:3], [sg_b, CB], [psb])
                    g = q if q < 8 else 8 + (q - 8) // 2
                    CP("act", carry_s[:, j, q * 4:(q + 1) * 4, :],
                       ps[:, 0:12].rearrange("p (a k) -> p a k", k=3), [psb], [carry_s_b[j][g]])

        mark("ssd_dt")
        wt, wbf = wfull(c_sdt[j], "sdt%d" % j, 16, 64)
        for c in range(nch):
            ps, psb = psA.next()
            for k in range(16):
                MM(ps[0:L, 0:64], xT[:, k, c * 128:c * 128 + L], wt[:, k, 0:64], k == 0, k == 15, [wbf] + xT_b, [psb])
            TT("dve", dtv[0:L, c, :], ps[0:L, 0:64], p_sdtb[0:L, j, :], ALU.add, [psb, PB], [dtv_b])
            ACT(dtv[0:L, c, :], dtv[0:L, c, :], AF.Exp, [dtv_b], [dtv_b])
            ACT(dtv[0:L, c, :], dtv[0:L, c, :], AF.Ln, [dtv_b], [dtv_b], bias=1.0)
            TT("dve", av[0:L, c, :], dtv[0:L, c, :], p_sA[0:L, j, :], ALU.mult, [dtv_b, PB], [av_b])
            ps2, ps2b = psA.next()
            MM(ps2[0:L, 0:64], mask01[0:L, 0:L], av[0:L, c, :], True, True, [av_b, CB], [ps2b])
            MM(ps2[:, 64:128], ones_f[0:L, :], av[0:L, c, :], True, True, [av_b, CB], [ps2b])
            CP("dve", cumv[0:L, c, :], ps2[0:L, 0:64], [ps2b], [cum_b])
            ACT(ev[0:L, c, :], ps2[0:L, 0:64], AF.Exp, [ps2b], [ev_b])
            ACT(decv[:, c, :], ps2[:, 64:128], AF.Exp, [ps2b], [dec_b])
            TT("dve", tlv[0:L, c, :], ps2[0:L, 64:128], cumv[0:L, c, :], ALU.subtract, [ps2b, cum_b], [tl_b])
            ACT(tlv[0:L, c, :], tlv[0:L, c, :], AF.Exp, [tl_b], [tl_b])

        for g in range(8):
            mark("ssd_g%d_z" % g)
            hbf, hbfb = hbf2[g % 2], hbf2_b[g % 2]
            CP("pool", hbf[:, :], hst[:, j, g * 512:(g + 1) * 512], [hst_b[j][g]], [hbfb])
            wt, wbf = wfull(c_sin[j, g], wn)
            for c in range(nch):
                ps, psb = psA.next()
                for k in range(16):
                    MM(ps[0:L, :], xT[:, k, c * 128:c * 128 + L], wt[:, k, :], k == 0, k == 15, [wbf] + xT_b, [psb])
                ACT(zs[0:L, c, :], ps[0:L, :], AF.Silu, [psb], [zs_b])
            mark("ssd_g%d_x" % g)
            wt, wbf = wfull(c_sin[j, 8 + g], wn)
            for ct in range(4):
                CP("pool", raw[:, ct, 0:3], carry_s[:, j, g * 4 + ct, :], [carry_s_b[j][g]], [raw_b])
            for ct in range(4):
                ps, psb = psA.next()
                for k in range(16):
                    MM(ps[:, 0:ntok], wt[:, k, ct * 128:(ct + 1) * 128], xT[:, k, 0:ntok], k == 0, k == 15,
                       [wbf] + xT_b, [psb])
                CP("act", raw[:, ct, 3:3 + ntok], ps[:, 0:ntok], [psb], [raw_b])
            wt2, wbf2 = wfull(c_sbc[j, g], wn, 16, 256)
            for bc in range(2):
                cti = 32 + g if bc == 0 else 40 + g
                CP("pool", raw[:, 4 + bc, 0:3], carry_s[:, j, cti, :], [carry_s_b[j][8 + bc]], [raw_b])
                ps, psb = psA.next()
                for k in range(16):
                    MM(ps[:, 0:ntok], wt2[:, k, bc * 128:(bc + 1) * 128], xT[:, k, 0:ntok], k == 0, k == 15,
                       [wbf2] + xT_b, [psb])
                CP("act", raw[:, 4 + bc, 3:3 + ntok], ps[:, 0:ntok], [psb], [raw_b])
            mark("ssd_g%d_conv" % g)
            for c6 in range(6):
                cti = g * 4 + c6 if c6 < 4 else (32 + g if c6 == 4 else 40 + g)
                conv_silu(raw[:, c6, :], raw_b, ntok,
                          (lambda cti: (lambda k: p_scw[:, j, k, cti:cti + 1]))(cti), p_scb[:, j, cti:cti + 1],
                          xc[:, c6, 0:ntok], xc_b, ctmp, ctmp_b)
                cb_ = carry_s_b[j][g] if c6 < 4 else carry_s_b[j][8 + (c6 - 4)]
                CP("pool", carry_s[:, j, cti, :], raw[:, c6, ntok:ntok + 3], [raw_b], [cb_])
            for c in range(nch):
                c0 = c * L
                hs = slice(g * 8, g * 8 + 8)
                mark("ssd_g%d_c%d" % (g, c))
                ps, psb = psB.next()
                for ct in range(4):
                    TR(ps[0:L, ct * 128:(ct + 1) * 128], xc[:, ct, c0:c0 + L], ident_bf[:], [xc_b, CB], [psb])
                TR(ps[0:L, 512:640], xc[:, 4, c0:c0 + L], ident_bf[:], [xc_b, CB], [psb])
                CP("act", xtok[0:L, :], ps[0:L, 0:512], [psb], [xtok_b])
                CP("dve", btok[0:L, :], ps[0:L, 512:640], [psb], [btok_b])
                x3 = xtok[0:L, :].rearrange("p (h q) -> p h q", q=64)
                TT("dve", xdt[0:L, :].rearrange("p (h q) -> p h q", q=64), x3,
                   dtv[0:L, c, hs].unsqueeze(2).to_broadcast([L, 8, 64]), ALU.mult, [xtok_b, dtv_b], [xdt_b])
                TT("dve", xD[0:L, :].rearrange("p (h q) -> p h q", q=64), x3,
                   p_sD[0:L, j, hs].unsqueeze(2).to_broadcast([L, 8, 64]), ALU.mult, [xtok_b, PB], [xD_b])
                TT("pool", xtl[0:L, :].rearrange("p (h q) -> p h q", q=64),
                   xdt[0:L, :].rearrange("p (h q) -> p h q", q=64),
                   tlv[0:L, c, hs].unsqueeze(2).to_broadcast([L, 8, 64]), ALU.mult, [xdt_b, tl_b], [xtl_b])
                mark("ssd_cb")
                ps, psb = psA.next()
                MM(ps[0:L, 0:L], xc[:, 4, c0:c0 + L], xc[:, 5, c0:c0 + L], True, True, [xc_b], [psb])
                TT("dve", cbm[0:L, 0:L], ps[0:L, 0:L], mask01[0:L, 0:L], ALU.mult, [psb, CB], [cbm_b])
                TT("dve", aV[0:L, :, 0:L], av[0:L, c, hs].unsqueeze(2).to_broadcast([L, 8, L]),
                   mask01[0:L, 0:L].unsqueeze(1).to_broadcast([L, 8, L]), ALU.mult, [av_b, CB], [aV_b])
                hp = min(8, 512 // L)
                for hb in range(0, 8, hp):
                    ps, psb = psA.next()
                    MM(ps[0:L, 0:hp * L], Umat[0:L, 0:L], aV[0:L, hb:hb + hp, 0:L], True, True, [aV_b, CB], [psb])
                    ACT(eM[0:L, hb:hb + hp, 0:L], ps[0:L, 0:hp * L].rearrange("p (h t) -> p h t", t=L), AF.Exp,
                        [psb], [eM_b])
                TT("dve", WT[0:L, :, 0:L], eM[0:L, :, 0:L], cbm[0:L, 0:L].unsqueeze(1).to_broadcast([L, 8, L]),
                   ALU.mult, [eM_b, cbm_b], [WT_b])
                mark("ssd_y")
                ps1, ps1b = psA.next()
                MM(ps1[0:L, :], ident_bf[0:L, 0:L], xD[0:L, :], True, False, [xD_b, CB], [ps1b], skip=True)
                for hh in range(8):
                    MM(ps1[0:L, hh * 64:(hh + 1) * 64], WT[0:L, hh, 0:L], xdt[0:L, hh * 64:(hh + 1) * 64],
                       False, hh == 7, [WT_b, xdt_b], [ps1b], skip=True)
                ps2, ps2b = psA.next()
                MM(ps2[0:L, :], xc[:, 5, c0:c0 + L], hbf[:, :], True, True,
                   [xc_b, hbfb], [ps2b])
                TT("dve", y2[0:L, :].rearrange("p (h q) -> p h q", q=64),
                   ps2[0:L, :].rearrange("p (h q) -> p h q", q=64),
                   ev[0:L, c, hs].unsqueeze(2).to_broadcast([L, 8, 64]), ALU.mult, [ps2b, ev_b], [y2_b])
                TT("dve", y1[0:L, :], ps1[0:L, :], y2[0:L, :], ALU.add, [ps1b, y2_b], [y1_b])
                mark("ssd_gate")
                TT("dve", y1[0:L, :], y1[0:L, :], zs[0:L, c, :], ALU.mult, [y1_b, zs_b], [y1_b])
                ACT(y2[0:L, :], y1[0:L, :], AF.Square, [y1_b], [y2_b, ss_b], accum=ss[0:L, 0:1])
                TS("dve", ss[0:L, 1:2], ss[0:L, 0:1], 1.0 / 512, RMS_EPS, ALU.mult, ALU.add, [ss_b], [ss_b])
                ACT(ss[0:L, 1:2], ss[0:L, 1:2], AF.Sqrt, [ss_b], [ss_b])
                S.add("dve", (lambda o, i_: (lambda e: e.reciprocal(out=o, in_=i_)))(ss[0:L, 2:3], ss[0:L, 1:2]),
                      [ss_b], [ss_b])
                ACT(yn[0:L, :], y1[0:L, :], AF.Copy, [y1_b, ss_b], [yn_b], scale=ss[0:L, 2:3])
                mark("ssd_back")
                ps, psb = psB.next()
                for ct in range(4):
                    TR(ps[:, ct * 128:ct * 128 + L], yn[0:L, ct * 128:(ct + 1) * 128], ident_bf[0:L, 0:L],
                       [yn_b, CB], [psb])
                for ct in range(4):
                    ACT(yT[:, g * 4 + ct, c0:c0 + L], ps[:, ct * 128:ct * 128 + L], AF.Copy, [psb, PB], [yT_b[g]],
                        scale=p_snw[:, j, g * 4 + ct:g * 4 + ct + 1])
                mark("ssd_state")
                ps, psb = psA.next()
                MM(ps[:, :], btok[0:L, :], xtl[0:L, :], True, True, [btok_b, xtl_b], [psb])
                hv = hst[:, j, g * 512:(g + 1) * 512]
                TT("dve", htmp[:, :].rearrange("p (h q) -> p h q", q=64), hv.rearrange("p (h q) -> p h q", q=64),
                   decv[:, c, hs].unsqueeze(2).to_broadcast([128, 8, 64]), ALU.mult, [hst_b[j][g], dec_b], [htmp_b])
                TT("dve", hv, htmp[:, :], ps[:, :], ALU.add, [htmp_b, psb], [hst_b[j][g]])
                if c < nch - 1:
                    CP("pool", hbf[:, :], hv, [hst_b[j][g]], [hbfb])

        mark("ssd_out")
        if lastblk:
            kind = 0 if seq["kind"] == "p" else 1
            dsth = o_sh[kind][j] if kind == 0 else o_sh[kind][j, seq["b"]]
            dstc = o_sc[kind][j] if kind == 0 else o_sc[kind][j, seq["b"]]
            dh = dsth.rearrange("h p n -> (h p) n")
            for q in range(8):
                ps, psb = psA.next()
                for a in range(4):
                    TR(ps[:, a * 128:(a + 1) * 128], hst[:, j, q * 512 + a * 128:q * 512 + (a + 1) * 128], ident_f[:],
                       [hst_b[j][q], CB], [psb])
                sg, sg_b = stg.next()
                CP("act", sg[:, :], ps[:, :], [psb], [sg_b])
                DMA("sp", dh[q * 512:(q + 1) * 512, :].rearrange("(a p) n -> p a n", p=128),
                    sg[:, :].rearrange("p (a n) -> p a n", n=128), [sg_b], [])
            for q in range(12):
                ps, psb = psA.next()
                g = q if q < 8 else 8 + (q - 8) // 2
                for a in range(4):
                    TR(ps[0:3, a * 128:(a + 1) * 128], carry_s[:, j, q * 4 + a, :], ident_f[:],
                       [carry_s_b[j][g], CB], [psb])
                sg, sg_b = stg.next()
                CP("act", sg[0:3, :], ps[0:3, :], [psb], [sg_b])
                DMA("sp", dstc[:, q * 512:(q + 1) * 512], sg[0:3, :], [sg_b], [])

    def ml_layer(j, ntok, L, first, lastblk, seq):
        nch = ntok // L
        areset()
        S.fence(dummies)
        wn = "min%d" % j
        li, li_b = aalloc("li", [ntok], F32, parts=8)
        lf, lf_b = aalloc("lf", [ntok], F32, parts=8)
        bb, bb_b = aalloc("bb", [ntok], F32, parts=8)
        uu, uu_b = aalloc("uu", [ntok], F32, parts=8)
        MR, MR_b = aalloc("MR", [ntok], F32, parts=8)
        rows, rows_b = aalloc("rows", [4, ntok], F32, parts=8)
        sc, sc_b = aalloc("sc", [nch, 4], F32, parts=8)
        gtok, gtok_b = aalloc("gtok", [nch, 4, 8], F32)
        tlbf, tlbf_b = aalloc("tlbf", [nch, 8], BF16)
        decb, decb_b = aalloc("decb", [nch, 8], F32)
        dg, dg_b = aalloc("dg", [8], F32, parts=8)
        raw, raw_b = aalloc("raw", [4, 3 + ntok], F32)
        ctmp, ctmp_b = aalloc("ctmp", [ntok], F32)
        xmb, xmb_b = aalloc("xmb", [4, ntok], BF16)
        xc, xc_b = aalloc("xc", [4, ntok], BF16)
        zs, zs_b = aalloc("zs", [4, ntok], BF16)
        so, so_b = aalloc("so", [4, ntok], BF16)
        qT, qT_b = aalloc("qT", [4, ntok], BF16)
        kT, kT_b = aalloc("kT", [4, ntok], BF16)
        vt, vt_b = aalloc("vt", [nch, 512], BF16)
        ktok, ktok_b = aalloc("ktok", [nch, 512], BF16)
        hnT, hnT_b = aalloc("hnT", [4, ntok], BF16)
        qkm, qkm_b = aalloc("qkm", [128], BF16)
        Asb, Asb_b = aalloc("Asb", [512], F32)
        num, num_b = aalloc("num", [512], F32)
        hn, hn_b = aalloc("hn", [512], BF16)
        dn, dn_b = aalloc("dn", [8], F32)
        st6, st6_b = aalloc("st6", [6], F32)
        t1, t1_b = aalloc("t1", [ntok], F32)

        kind = 0 if seq["kind"] == "p" else 1
        if first:
            if kind == 0:
                for h in range(8):
                    MS("pool", nst[:, j, h, :], 0.0, [nst_b[j][h]])
                    MS("pool", carry_m[:, j, h * 4:(h + 1) * 4, :], 0.0, [carry_m_b[j][h]])
                MS("pool", mst[:, j:j + 1], 0.0, [mst_b[j]])
            else:
                b = seq["b"]
                for h in range(8):
                    DMA("sp", nst[:, j, h, :], i_mn[j, b, h].rearrange("(e p) -> p e", p=128), [], [nst_b[j][h]],
                        slow=True)
                DMA("sp", mst[:, j:j + 1], i_mm[j, b].rearrange("(p o) -> p o", o=1), [], [mst_b[j]], slow=True)
                for q in range(8):
                    sg, sg_b = stg.next()
                    DMA("sp", sg[0:3, :], i_mc[j, b][:, q * 512:(q + 1) * 512], [], [sg_b])
                    ps, psb = psA.next()
                    for a in range(4):
                        TR(ps[:, a * 3:a * 3 + 3], sg[0:3, a * 128:(a + 1) * 128], ident_f[0:3, 0:3], [sg_b, CB], [psb])
                    CP("act", carry_m[:, j, q * 4:(q + 1) * 4, :],
                       ps[:, 0:12].rearrange("p (a k) -> p a k", k=3), [psb], [carry_m_b[j][q]])

        wt, wbf = wfull(c_mg[j], "mg%d" % j, 16, 16)
        ps, psb = psA.next()
        for k in range(16):
            MM(ps[0:8, 0:ntok], wt[:, k, 0:8], xT[:, k, 0:ntok], k == 0, k == 15, [wbf] + xT_b, [psb])
        ACT(li[:, :], ps[0:8, 0:ntok], AF.Identity, [psb, PB], [li_b], bias=p_mbi[:, j:j + 1])
        ps, psb = psA.next()
        for k in range(16):
            MM(ps[0:8, 0:ntok], wt[:, k, 8:16], xT[:, k, 0:ntok], k == 0, k == 15, [wbf] + xT_b, [psb])
        ACT(lf[:, :], ps[0:8, 0:ntok], AF.Exp, [psb, PB], [lf_b], bias=p_mnbf[:, j:j + 1], scale=-1.0)
        ACT(lf[:, :], lf[:, :], AF.Ln, [lf_b], [lf_b], bias=1.0)
        TS("dve", lf[:, :], lf[:, :], -1.0, None, ALU.mult, None, [lf_b], [lf_b])
        for c in range(nch):
            cs = slice(c * L, (c + 1) * L)
            mprev = mst[:, j:j + 1]
            S.add("dve", (lambda o, d0, d1: (lambda e: e.tensor_tensor_scan(
                out=o, data0=d0, data1=d1, initial=0.0, op0=ALU.add, op1=ALU.add)))(bb[:, cs], lf[:, cs], zeros_f[0:8, 0:L]),
                [lf_b, CB], [bb_b])
            TT("dve", uu[:, cs], li[:, cs], bb[:, cs], ALU.subtract, [li_b, bb_b], [uu_b])
            S.add("dve", (lambda o, d0, ini: (lambda e: e.tensor_tensor_scan(
                out=o, data0=d0, data1=d0, initial=ini, op0=ALU.max, op1=ALU.max)))(MR[:, cs], uu[:, cs], mprev),
                [uu_b, mst_b[j]], [MR_b])
            last1 = slice((c + 1) * L - 1, (c + 1) * L)
            TS("dve", sc[:, c, 0:1], MR[:, last1], -1.0, None, ALU.mult, None, [MR_b], [sc_b])
            CP("dve", sc[:, c, 1:2], MR[:, last1], [MR_b], [sc_b])
            CP("dve", sc[:, c, 2:3], mprev, [mst_b[j]], [sc_b])
            ACT(rows[:, 0, cs], uu[:, cs], AF.Exp, [uu_b, sc_b], [rows_b], bias=sc[:, c, 0:1])
            ACT(rows[:, 1, cs], MR[:, cs], AF.Exp, [MR_b, sc_b], [rows_b], bias=sc[:, c, 1:2], scale=-1.0)
            ACT(rows[:, 2, cs], MR[:, cs], AF.Exp, [MR_b, sc_b], [rows_b], bias=sc[:, c, 2:3], scale=-1.0)
            TT("dve", rows[:, 3, cs], bb[:, cs], MR[:, cs], ALU.add, [bb_b, MR_b], [rows_b])
            CP("dve", mst[:, j:j + 1], rows[:, 3, last1], [rows_b, sc_b], [mst_b[j]])
            ACT(rows[:, 3, cs], rows[:, 3, cs], AF.Exp, [rows_b, mst_b[j]], [rows_b], scale=-1.0)
            ps, psb = psA.next()
            for a in range(4):
                TR(ps[0:L, a * 8:(a + 1) * 8], rows[:, a, cs], ident_f[0:8, 0:8], [rows_b, CB], [psb])
            CP("dve", gtok[0:L, c, :, :], ps[0:L, 0:32].rearrange("p (a h) -> p a h", h=8), [psb], [gtok_b])
            CP("pool", tlbf[0:L, c, :], gtok[0:L, c, 0, :], [gtok_b], [tlbf_b])
            TS("dve", dg[:, :], ident_f[0:8, 0:8], rows[:, 2, last1], None, ALU.mult, None, [rows_b, CB], [dg_b])
            ps, psb = psA.next()
            MM(ps[:, 0:8], ones_f[0:8, :], dg[:, :], True, True, [dg_b, CB], [psb])
            CP("dve", decb[:, c, :], ps[:, 0:8], [psb], [decb_b])

        pend = []
        for h in range(8):
            Ct, Ct_b = Cst.next()
            if first:
                if kind == 0:
                    MS("pool", Ct[:], 0.0, [Ct_b])
                else:
                    DMA("sp", Ct[:], i_mC[j, seq["b"], h].rearrange("(k p) e -> p k e", p=128), [], [Ct_b])
            else:
                DMA("sp", Ct[:], c_scr[j, h].rearrange("(k p) e -> p k e", p=128), [CSB[j][h]], [Ct_b])
            CP("act", Cbf[:], Ct[:], [Ct_b], [Cbf_b])
            CP("pool", nbf[:, :], nst[:, j, h, :], [nst_b[j][h]], [nbf_b])
            for which in range(3):
                wt, wbf = wfull(c_min[j, which * 8 + h], wn)
                if which == 0:
                    while pend:
                        a_ = pend.pop(0)
                        DMA("sp", a_[0], a_[1], a_[2], a_[3])
                if which == 0:
                    for ct in range(4):
                        CP("pool", raw[:, ct, 0:3], carry_m[:, j, h * 4 + ct, :], [carry_m_b[j][h]], [raw_b])
                for ct in range(4):
                    ps, psb = psA.next()
                    for k in range(16):
                        MM(ps[:, 0:ntok], wt[:, k, ct * 128:(ct + 1) * 128], xT[:, k, 0:ntok], k == 0, k == 15,
                           [wbf] + xT_b, [psb])
                    if which == 0:
                        CP("act", raw[:, ct, 3:3 + ntok], ps[:, 0:ntok], [psb], [raw_b])
                        CP("dve", xmb[:, ct, 0:ntok], ps[:, 0:ntok], [psb], [xmb_b])
                    elif which == 1:
                        ACT(zs[:, ct, 0:ntok], ps[:, 0:ntok], AF.Silu, [psb], [zs_b])
                    else:
                        ACT(so[:, ct, 0:ntok], ps[:, 0:ntok], AF.Sigmoid, [psb], [so_b])
            for ct in range(4):
                cti = h * 4 + ct
                conv_silu(raw[:, ct, :], raw_b, ntok,
                          (lambda cti: (lambda k: p_mcw[:, j, k, cti:cti + 1]))(cti), p_mcb[:, j, cti:cti + 1],
                          xc[:, ct, 0:ntok], xc_b, ctmp, ctmp_b)
                CP("pool", carry_m[:, j, cti, :], raw[:, ct, ntok:ntok + 3], [raw_b], [carry_m_b[j][h]])
            wt, wbf = wfull(c_mqkv[j, h], "mqkv%d" % j, 12, 512)
            for et in range(4):
                ps, psb = psA.next()
                for k in range(4):
                    MM(ps[:, 0:ntok], wt[:, k, et * 128:(et + 1) * 128], xc[:, k, 0:ntok], k == 0, k == 3,
                       [wbf, xc_b], [psb])
                ACT(qT[:, et, 0:ntok], ps[:, 0:ntok], AF.Copy, [psb], [qT_b], scale=float(512 ** -0.5))
                ps, psb = psA.next()
                for k in range(4):
                    MM(ps[:, 0:ntok], wt[:, 4 + k, et * 128:(et + 1) * 128], xc[:, k, 0:ntok], k == 0, k == 3,
                       [wbf, xc_b], [psb])
                CP("dve", kT[:, et, 0:ntok], ps[:, 0:ntok], [psb], [kT_b])
            for c in range(nch):
                cs = slice(c * L, (c + 1) * L)
                ps, psb = psA.next()
                for k in range(4):
                    MM(ps[0:L, :], xmb[:, k, cs], wt[:, 8 + k, :], k == 0, k == 3, [wbf, xmb_b], [psb])
                ACT(vt[0:L, c, :], ps[0:L, :], AF.Copy, [psb, gtok_b], [vt_b], scale=gtok[0:L, c, 0, h:h + 1])
                ps, psb = psA.next()
                for k in range(4):
                    MM(ps[0:L, :], xc[:, k, cs], wt[:, 4 + k, :], k == 0, k == 3, [wbf, xc_b], [psb])
                CP("dve", ktok[0:L, c, :], ps[0:L, :], [psb], [ktok_b])
            for c in range(nch):
                cs = slice(c * L, (c + 1) * L)
                ps, psb = psA.next()
                for k in range(4):
                    MM(ps[0:L, 0:L], kT[:, k, cs], qT[:, k, cs], k == 0, k == 3, [kT_b, qT_b], [psb])
                TT("dve", qkm[0:L, 0:L], ps[0:L, 0:L], mask01[0:L, 0:L], ALU.mult, [psb, CB], [qkm_b])
                psa, psab = psA.next()
                MM(psa[0:L, :], qkm[0:L, 0:L], vt[0:L, c, :], True, True, [qkm_b, vt_b], [psab])
                psb2, psb2b = psA.next()
                for k in range(4):
                    MM(psb2[0:L, :], qT[:, k, cs], Cbf[:, k, :], k == 0, k == 3, [qT_b, Cbf_b], [psb2b])
                psd, psdb = psA.next()
                MM(psd[0:L, 0:1], qkm[0:L, 0:L], tlbf[0:L, c, h:h + 1], True, True, [qkm_b, tlbf_b], [psdb])
                for k in range(4):
                    MM(psd[0:L, 1:2], qT[:, k, cs], nbf[:, k:k + 1], k == 0, k == 3, [qT_b, nbf_b], [psdb])
                ACT(Asb[0:L, :], psa[0:L, :], AF.Copy, [psab, gtok_b], [Asb_b], scale=gtok[0:L, c, 1, h:h + 1])
                STT(num[0:L, :], psb2[0:L, :], gtok[0:L, c, 2, h:h + 1], Asb[0:L, :], ALU.mult, ALU.add,
                    [psb2b, gtok_b, Asb_b], [num_b])
                TT("dve", dn[0:L, 0:2], psd[0:L, 0:2], gtok[0:L, c, 1:3, h], ALU.mult, [psdb, gtok_b], [dn_b])
                TT("dve", dn[0:L, 2:3], dn[0:L, 0:1], dn[0:L, 1:2], ALU.add, [dn_b], [dn_b])
                TS("dve", dn[0:L, 3:4], dn[0:L, 2:3], -1.0, None, ALU.mult, None, [dn_b], [dn_b])
                TT("dve", dn[0:L, 3:4], dn[0:L, 3:4], dn[0:L, 2:3], ALU.max, [dn_b], [dn_b])
                TT("dve", dn[0:L, 4:5], dn[0:L, 3:4], gtok[0:L, c, 3, h:h + 1], ALU.max, [dn_b, gtok_b], [dn_b])
                S.add("dve", (lambda o, i_: (lambda e: e.reciprocal(out=o, in_=i_)))(dn[0:L, 5:6], dn[0:L, 4:5]),
                      [dn_b], [dn_b])
                TS("dve", num[0:L, :], num[0:L, :], dn[0:L, 5:6], None, ALU.mult, None, [num_b, dn_b], [num_b])
                S.add("dve", (lambda o, i_: (lambda e: e.bn_stats(out=o, in_=i_)))(st6[0:L, 0:6], num[0:L, :]),
                      [num_b], [st6_b])
                S.add("dve", (lambda o, i_: (lambda e: e.bn_aggr(out=o, in_=i_)))(dn[0:L, 6:8], st6[0:L, 0:6]),
                      [st6_b], [dn_b])
                TS("dve", dn[0:L, 7:8], dn[0:L, 7:8], LN_EPS, None, ALU.add, None, [dn_b], [dn_b])
                ACT(dn[0:L, 7:8], dn[0:L, 7:8], AF.Sqrt, [dn_b], [dn_b])
                S.add("dve", (lambda o, i_: (lambda e: e.reciprocal(out=o, in_=i_)))(dn[0:L, 7:8], dn[0:L, 7:8]),
                      [dn_b], [dn_b])
                TS("dve", hn[0:L, :], num[0:L, :], dn[0:L, 6:7], dn[0:L, 7:8], ALU.subtract, ALU.mult,
                   [num_b, dn_b], [hn_b])
                ps, psb = psB.next()
                for et in range(4):
                    TR(ps[:, et * 128:et * 128 + L], hn[0:L, et * 128:(et + 1) * 128], ident_bf[0:L, 0:L],
                       [hn_b, CB], [psb])
                CP("act", hnT[:, :, cs], ps[:, 0:512].rearrange("p (a t) -> p a t", t=128)[:, :, 0:L], [psb], [hnT_b])
                psn, psnb = psA.next()
                for kt_ in range(4):
                    MM(psn[:, kt_:kt_ + 1], ktok[0:L, c, kt_ * 128:(kt_ + 1) * 128], tlbf[0:L, c, h:h + 1], True, True,
                       [ktok_b, tlbf_b], [psnb])
                STT(nst[:, j, h, :], nst[:, j, h, :], decb[:, c, h:h + 1], psn[:, 0:4], ALU.mult, ALU.add,
                    [psnb, decb_b, nst_b[j][h]], [nst_b[j][h]])
                CP("pool", nbf[:, :], nst[:, j, h, :], [nst_b[j][h]], [nbf_b])
                for kt_ in range(4):
                    ps, psb = psA.next()
                    MM(ps[:, :], ktok[0:L, c, kt_ * 128:(kt_ + 1) * 128], vt[0:L, c, :], True, True,
                       [ktok_b, vt_b], [psb])
                    STT(Ct[:, kt_, :], Ct[:, kt_, :], decb[:, c, h:h + 1], ps[:, :], ALU.mult, ALU.add,
                        [psb, decb_b, Ct_b], [Ct_b])
                if c < nch - 1:
                    CP("act", Cbf[:], Ct[:], [Ct_b], [Cbf_b])
            if lastblk:
                dstC = o_mC[kind][j, h] if kind == 0 else o_mC[kind][j, seq["b"], h]
                pend.append((dstC.rearrange("(k p) e -> p k e", p=128), Ct[:], [Ct_b], []))
            else:
                pend.append((c_scr[j, h].rearrange("(k p) e -> p k e", p=128), Ct[:], [Ct_b], [CSB[j][h]]))
            for et in range(4):
                cti = h * 4 + et
                STT(t1[:, 0:ntok], hnT[:, et, 0:ntok], p_mnw[:, j, cti:cti + 1], so[:, et, 0:ntok], ALU.mult, ALU.mult,
                    [hnT_b, so_b, PB], [t1_b])
                STT(t1[:, 0:ntok], xc[:, et, 0:ntok], p_msk[:, j, cti:cti + 1], t1[:, 0:ntok], ALU.mult, ALU.add,
                    [xc_b, t1_b, PB], [t1_b])
                TT("dve", yT[:, cti, 0:ntok], t1[:, 0:ntok], zs[:, et, 0:ntok], ALU.mult, [t1_b, zs_b], [yT_b[h]])

        while pend:
            a_ = pend.pop(0)
            DMA("sp", a_[0], a_[1], a_[2], a_[3])
        if lastblk:
            dn_ = o_mn[kind][j] if kind == 0 else o_mn[kind][j, seq["b"]]
            dm_ = o_mm[kind][j] if kind == 0 else o_mm[kind][j, seq["b"]]
            dc_ = o_mc[kind][j] if kind == 0 else o_mc[kind][j, seq["b"]]
            for h in range(8):
                DMA("sp", dn_[h].rearrange("(e p) -> p e", p=128), nst[:, j, h, :], [nst_b[j][h]], [], slow=True)
            DMA("sp", dm_.rearrange("(p o) -> p o", o=1), mst[:, j:j + 1], [mst_b[j]], [], slow=True)
            for q in range(8):
                ps, psb = psA.next()
                for a in range(4):
                    TR(ps[0:3, a * 128:(a + 1) * 128], carry_m[:, j, q * 4 + a, :], ident_f[:],
                       [carry_m_b[j][q], CB], [psb])
                sg, sg_b = stg.next()
                CP("act", sg[0:3, :], ps[0:3, :], [psb], [sg_b])
                DMA("sp", dc_[:, q * 512:(q + 1) * 512], sg[0:3, :], [sg_b], [])

    CSB = [[Buf("cscr%d_%d" % (j, h)) for h in range(8)] for j in range(2)]
    qk_all = Buf("qkv_all")

    seqs = []
    if NPB > 0:
        seqs.append({"kind": "p", "T": TP, "ntok": NT, "L": 128, "nblk": NPB})
    for b in range(NS):
        seqs.append({"kind": "s", "b": b, "T": 16, "ntok": 16, "L": 16, "nblk": 1})
    import os as _os
    _dbg = _os.environ.get("KDBG", "")
    if _dbg == "conv":
        seqs = []
    if _os.environ.get("KNOSAMPLE", "0") == "1":
        seqs = [q for q in seqs if q["kind"] == "p"]
    if _os.environ.get("KNOPROMPT", "0") == "1":
        seqs = [q for q in seqs if q["kind"] == "s"]
    for seq in seqs:
        ntok, L = seq["ntok"], seq["L"]
        TL = min(128, ntok)
        ntt = max(1, ntok // 128)
        for blk in range(seq["nblk"]):
            first = blk == 0
            lastblk = blk == seq["nblk"] - 1
            if seq["kind"] == "p":
                xsrc = xp[blk * NT:(blk + 1) * NT, :]
                ydst = yp[blk * NT:(blk + 1) * NT, :]
            else:
                xsrc = xs[seq["b"]]
                ydst = ys[seq["b"]]
            for t in range(ntt):
                DMA("sp", xres[0:TL, t, :], xsrc[t * 128:t * 128 + TL, :], [], [xres_b[t]])
            _nl = int(_os.environ.get("KLAYERS", "4"))
            if _nl < DEPTH:
                if _nl == 0 and _os.environ.get("KXT", "0") == "1":
                    make_xT(ntok, TL, ntt)
                for t in range(ntt):
                    DMA("sp", ydst[t * 128:t * 128 + TL, :], xres[0:TL, t, :], [xres_b[t]] + xT_b, [])
            for i in range(_nl):
                j = i // 2
                make_xT(ntok, TL, ntt)
                if i % 2 == 0:
                    ssd_layer(j, ntok, L, first, lastblk, seq)
                    out_proj_ln(i, c_sout[j], "sout%d" % j, ntok, TL, ntt, i == _nl - 1, ydst)
                else:
                    ml_layer(j, ntok, L, first, lastblk, seq)
                    out_proj_ln(i, c_mout[j], "mout%d" % j, ntok, TL, ntt, i == _nl - 1, ydst)

    S.emit(nc, es)
    es.close()
    return nc, len(S.ops)


_CACHE = {}


def _get_program(NPB, NS, NT):
    key = (NPB, NS, NT)
    if key not in _CACHE:
        _CACHE[key] = build(NPB, NS, NT)
    return _CACHE[key]


WEIGHT_KEYS = ["ssd_w_in", "ssd_conv_w", "ssd_conv_b", "ssd_dt_bias", "ssd_A_log", "ssd_D", "ssd_norm_w", "ssd_w_out",
               "ml_w_in", "ml_conv_w", "ml_conv_b", "ml_w_q", "ml_w_k", "ml_w_v", "ml_b_i", "ml_b_f", "ml_norm_w",
               "ml_skip", "ml_w_out", "ln_g", "ln_b"]


def kernel(NT=256, **inp):
    f = lambda a: np.ascontiguousarray(np.asarray(a, dtype=np.float32))
    xp_all = f(inp["x_prompt"])
    xs_all = f(inp["x_sample"])
    B, T, _ = xp_all.shape
    DB = xs_all.shape[0]
    assert DB % NCORES == 0 and T % NT == 0 and B <= NCORES
    NS = DB // NCORES
    NPB = T // NT
    nc, nops = _get_program(NPB, NS, NT)
    wts = {k: f(inp[k]) for k in WEIGHT_KEYS}
    st = {k: f(inp[k]) for k in ["state_ssd_conv", "state_ssd_h", "state_mlstm_conv", "state_mlstm_C",
                                 "state_mlstm_n", "state_mlstm_m"]}
    in_maps = []
    for c in range(NCORES):
        sl = slice(c * NS, (c + 1) * NS)
        m = {"xp": xp_all[c % B], "xs": xs_all[sl],
             "i_sc": np.ascontiguousarray(st["state_ssd_conv"][:, sl]),
             "i_sh": np.ascontiguousarray(st["state_ssd_h"][:, sl]),
             "i_mc": np.ascontiguousarray(st["state_mlstm_conv"][:, sl]),
             "i_mC": np.ascontiguousarray(st["state_mlstm_C"][:, sl]),
             "i_mn": np.ascontiguousarray(st["state_mlstm_n"][:, sl]),
             "i_mm": np.ascontiguousarray(st["state_mlstm_m"][:, sl])}
        m.update(wts)
        in_maps.append(m)
    if os.environ.get("KTRACE", "0") == "1":
        res = run_bass_kernel_spmd(nc, in_maps, core_ids=list(range(NCORES)), trace=True)
        print("EXEC_TIME_NS", res.exec_time_ns, flush=True)
        try:
            import collections
            insts = res.instructions_and_trace[0]
            t0 = min(i.timestamp for i in insts); t1 = max(i.end_timestamp for i in insts)
            lo = t0 + (t1 - t0) * float(os.environ.get("KWIN0", "0.6")); hi = t0 + (t1 - t0) * float(os.environ.get("KWIN1", "0.95"))
            print("window ms", (hi - lo) / 1e6)
            agg = collections.defaultdict(lambda: [0, 0])
            aggl = collections.defaultdict(lambda: [0, 0])
            for i in insts:
                if i.timestamp < lo or i.timestamp > hi:
                    continue
                k = (i.engine, i.name)
                agg[k][0] += i.duration; agg[k][1] += 1
                kl = (i.engine, i.name, i.source_line)
                aggl[kl][0] += i.duration; aggl[kl][1] += 1
            for k, v in sorted(agg.items(), key=lambda kv: -kv[1][0])[:28]:
                print("AGG %-10s %-28s %9.1f us n=%d" % (k[0], k[1], v[0] / 1e3, v[1]))
            for k, v in sorted(aggl.items(), key=lambda kv: -kv[1][0])[:40]:
                print("LINE %-10s %-24s L%-5s %9.1f us n=%d" % (k[0], k[1], k[2], v[0] / 1e3, v[1]))
        except Exception as e_:
            print("trace agg failed", e_)
    else:
        res = run_bass_kernel_spmd(nc, in_maps, core_ids=list(range(NCORES)))
    R = res.results
    y_prompt = np.stack([R[c]["yp"] for c in range(B)], 0)
    y_sample = np.concatenate([R[c]["ys"] for c in range(NCORES)], 0)

    def pst(name):
        return np.stack([R[c][name] for c in range(B)], 1)

    def sst(name):
        return np.concatenate([R[c][name] for c in range(NCORES)], 1)

    outs = (y_prompt, y_sample, pst("p_sc"), pst("p_sh"), pst("p_mc"), pst("p_mC"), pst("p_mn"), pst("p_mm"),
            sst("s_sc"), sst("s_sh"), sst("s_mc"), sst("s_mC"), sst("s_mn"), sst("s_mm"))
    return tuple(np.ascontiguousarray(o, dtype=np.float32) for o in outs)
```

```python
import os
import numpy as np
import concourse.bass as bass
import concourse.mybir as mybir
from concourse.bass_utils import run_bass_kernel_spmd
from contextlib import ExitStack

F32 = mybir.dt.float32
BF16 = mybir.dt.bfloat16
AF = mybir.ActivationFunctionType
ALU = mybir.AluOpType

D = 2048
KT = 16
DEPTH = 4
ALPHA = float((2 * DEPTH) ** 0.25)
LN_EPS = 1e-5
RMS_EPS = 1e-6
S_PROJ = 10304
M_PROJ = 12304
SAME_ENG_SYNC = os.environ.get("KSAME", "1") == "1"
import os
NCORES = int(os.environ.get('KCORES', '8'))


class Buf:
    __slots__ = ("name", "lw", "rd", "excl")

    def __init__(self, name, excl=False):
        self.name = name
        self.lw = []
        self.rd = []
        self.excl = excl


class Op:
    __slots__ = ("eng", "fn", "deps", "dma", "sig", "cnt", "semi")

    def __init__(self, eng, fn, deps, dma):
        self.eng = eng
        self.fn = fn
        self.deps = deps
        self.dma = dma
        self.sig = False
        self.cnt = 0
        self.semi = 0


class Sched:
    ENGS = ["pe", "act", "dve", "pool", "sp"]
    RING = 12

    def __init__(self):
        self.ops = []
        self.eng_ops = {e: [] for e in self.ENGS}
        self.last = {e: None for e in self.ENGS}

    MAXOPS = int(os.environ.get("KMAXOPS", "100000000"))

    def add(self, eng, fn, r=(), w=(), dma=False, conc=False, extra=()):
        i = len(self.ops)
        if i >= self.MAXOPS:
            return None
        deps = set(extra)
        if any(b.excl for b in r):
            w = list(w) + [b for b in r if b.excl and b not in w]
            r = [b for b in r if not b.excl]
        for b in r:
            deps.update(b.lw)
        for b in w:
            if not conc:
                deps.update(b.lw)
            deps.update(b.rd)
        for b in r:
            b.rd.append(i)
        for b in w:
            if conc:
                b.lw.append(i)
            else:
                b.lw = [i]
            b.rd = []
        deps.discard(i)
        op = Op(eng, fn, deps, dma)
        self.ops.append(op)
        self.eng_ops[eng].append(i)
        self.last[eng] = i
        return i

    def fence(self, dummies):
        lasts = [self.last[e] for e in ("pe", "act", "dve", "pool") if self.last[e] is not None]
        for e in ("act", "dve", "pool"):
            t = dummies[e]
            if e == "act":
                z = dummies["zsrc"]
                self.add(e, (lambda t, z: (lambda en: en.activation(out=t, in_=z, func=AF.Copy)))(t, z),
                         r=[dummies["zbuf"]], extra=lasts)
            else:
                self.add(e, (lambda t: (lambda en: en.memset(t, 0.0)))(t), extra=lasts)

    def emit(self, nc, es):
        ops = self.ops
        comp = ("act", "dve", "pool")
        for op in ops:
            for d in op.deps:
                t = ops[d]
                if t.dma:
                    continue
                if t.eng != op.eng or (SAME_ENG_SYNC and t.eng in comp):
                    t.sig = True
        ndma = {}
        for e in self.ENGS:
            c = 0
            k = 0
            for i in self.eng_ops[e]:
                op = ops[i]
                if op.dma:
                    op.semi = k % self.RING
                    op.cnt = 16 * (k // self.RING + 1)
                    k += 1
                else:
                    if op.sig:
                        c += 1
                    op.cnt = c
            ndma[e] = k
        sem_e = {e: es.enter_context(nc.semaphore("sem_" + e)) for e in ("pe", "act", "dve", "pool")}
        sem_d = {}
        for e in self.ENGS:
            if ndma[e] > 0:
                sem_d[e] = [es.enter_context(nc.semaphore("semd_%s_%d" % (e, j)))
                            for j in range(min(self.RING, ndma[e]))]
        block = es.enter_context(nc.Block())

        def run(e, eng):
            waited = {}
            for i in self.eng_ops[e]:
                op = ops[i]
                need = {}
                for d in op.deps:
                    t = ops[d]
                    if t.dma:
                        key = ("d", t.eng, t.semi)
                    else:
                        if t.eng == e and not (SAME_ENG_SYNC and e in comp):
                            continue
                        key = ("e", t.eng)
                    if t.cnt > need.get(key, 0):
                        need[key] = t.cnt
                if op.dma and op.cnt > 16:
                    key = ("d", e, op.semi)
                    need[key] = max(need.get(key, 0), op.cnt - 16)
                for key, val in need.items():
                    if waited.get(key, 0) < val:
                        sem = sem_e[key[1]] if key[0] == "e" else sem_d[key[1]][key[2]]
                        eng.wait_ge(sem, val)
                        waited[key] = val
                ins = op.fn(eng)
                if op.dma:
                    ins.then_inc(sem_d[e][op.semi], 16)
                elif op.sig:
                    ins.then_inc(sem_e[e], 1)
            if ndma[e] > 0:
                k = ndma[e]
                for j in range(min(self.RING, k)):
                    n_on = (k - 1 - j) // self.RING + 1
                    if waited.get(("d", e, j), 0) < 16 * n_on:
                        eng.wait_ge(sem_d[e][j], 16 * n_on)

        @block.tensor
        def _(eng):
            run("pe", eng)

        @block.scalar
        def _(eng):
            run("act", eng)

        @block.vector
        def _(eng):
            run("dve", eng)

        @block.gpsimd
        def _(eng):
            run("pool", eng)

        @block.sync
        def _(eng):
            run("sp", eng)


class Ring:
    def __init__(self, items):
        self.items = items
        self.i = 0

    def next(self):
        it = self.items[self.i % len(self.items)]
        self.i += 1
        return it


def build(NPB, NS, NT):
    nc = bass.Bass("TRN2", target_bir_lowering=False)
    S = Sched()
    es = ExitStack()
    TP = NPB * NT
    NTT = NT // 128

    def din(name, shape):
        return nc.dram_tensor(name, list(shape), F32, kind="ExternalInput").ap()

    def dout(name, shape):
        return nc.dram_tensor(name, list(shape), F32, kind="ExternalOutput").ap()

    def dscr(name, shape, dt):
        return nc.dram_tensor(name, list(shape), dt, kind="Internal").ap()

    xp = din("xp", [TP, D])
    xs = din("xs", [NS, 16, D])
    i_sc = din("i_sc", [2, NS, 3, 6144])
    i_sh = din("i_sh", [2, NS, 64, 64, 128])
    i_mc = din("i_mc", [2, NS, 3, 4096])
    i_mC = din("i_mC", [2, NS, 8, 512, 512])
    i_mn = din("i_mn", [2, NS, 8, 512])
    i_mm = din("i_mm", [2, NS, 8])
    w_sin = din("ssd_w_in", [2, D, S_PROJ])
    w_scw = din("ssd_conv_w", [2, 4, 6144])
    w_scb = din("ssd_conv_b", [2, 6144])
    w_sdtb = din("ssd_dt_bias", [2, 64])
    w_sA = din("ssd_A_log", [2, 64])
    w_sD = din("ssd_D", [2, 64])
    w_snw = din("ssd_norm_w", [2, 4096])
    w_sout = din("ssd_w_out", [2, 4096, D])
    w_min = din("ml_w_in", [2, D, M_PROJ])
    w_mcw = din("ml_conv_w", [2, 4, 4096])
    w_mcb = din("ml_conv_b", [2, 4096])
    w_mq = din("ml_w_q", [2, 8, 512, 512])
    w_mk = din("ml_w_k", [2, 8, 512, 512])
    w_mv = din("ml_w_v", [2, 8, 512, 512])
    w_mbi = din("ml_b_i", [2, 8])
    w_mbf = din("ml_b_f", [2, 8])
    w_mnw = din("ml_norm_w", [2, 8, 512])
    w_msk = din("ml_skip", [2, 4096])
    w_mout = din("ml_w_out", [2, 4096, D])
    w_lng = din("ln_g", [4, D])
    w_lnb = din("ln_b", [4, D])

    yp = dout("yp", [TP, D])
    ys = dout("ys", [NS, 16, D])
    o_sc = [dout("p_sc", [2, 3, 6144]), dout("s_sc", [2, NS, 3, 6144])]
    o_sh = [dout("p_sh", [2, 64, 64, 128]), dout("s_sh", [2, NS, 64, 64, 128])]
    o_mc = [dout("p_mc", [2, 3, 4096]), dout("s_mc", [2, NS, 3, 4096])]
    o_mC = [dout("p_mC", [2, 8, 512, 512]), dout("s_mC", [2, NS, 8, 512, 512])]
    o_mn = [dout("p_mn", [2, 8, 512]), dout("s_mn", [2, NS, 8, 512])]
    o_mm = [dout("p_mm", [2, 8]), dout("s_mm", [2, NS, 8])]

    c_sin = dscr("c_sin", [2, 16, 128, 16 * 512], BF16)
    c_sbc = dscr("c_sbc", [2, 8, 128, 16 * 256], BF16)
    c_sdt = dscr("c_sdt", [2, 128, 16 * 64], BF16)
    c_sout = dscr("c_sout", [2, 8, 128, 16 * 512], BF16)
    c_min = dscr("c_min", [2, 24, 128, 16 * 512], BF16)
    c_mg = dscr("c_mg", [2, 128, 16 * 16], BF16)
    c_mout = dscr("c_mout", [2, 8, 128, 16 * 512], BF16)
    c_mqkv = dscr("c_mqkv", [2, 8, 128, 12 * 512], BF16)
    c_scr = dscr("c_scr", [2, 8, 512, 512], F32)
    WB = {}

    def cvt(name, dst, src, C):
        if name not in WB:
            WB[name] = Buf("wb_" + name)
        o = dst.rearrange("p (k c) -> p k c", c=C)
        i_ = src.rearrange("(k p) c -> p k c", p=128)
        S.add("pool", (lambda o, i_: (lambda e: e.dma_start(out=o, in_=i_)))(o, i_), w=[WB[name]], dma=True, conc=True)

    def sb(name, shape, dt):
        return es.enter_context(nc.sbuf_tensor(name, list(shape), dt))

    ident_bf = sb("ident_bf", [128, 128], BF16)
    ident_f = sb("ident_f", [128, 128], F32)
    mask01 = sb("mask01", [128, 128], F32)
    Umat = sb("Umat", [128, 128], F32)
    ones_f = sb("ones_f", [128, 128], F32)
    zeros_f = sb("zeros_f", [128, 128], F32)
    CB = Buf("consts")

    p_scw = sb("p_scw", [128, 2, 4, 48], F32)
    p_scb = sb("p_scb", [128, 2, 48], F32)
    p_snw = sb("p_snw", [128, 2, 32], F32)
    p_sdtb = sb("p_sdtb", [128, 2, 64], F32)
    p_sA = sb("p_sA", [128, 2, 64], F32)
    p_sD = sb("p_sD", [128, 2, 64], F32)
    p_mcw = sb("p_mcw", [128, 2, 4, 32], F32)
    p_mcb = sb("p_mcb", [128, 2, 32], F32)
    p_mnw = sb("p_mnw", [128, 2, 32], F32)
    p_msk = sb("p_msk", [128, 2, 32], F32)
    p_mbi = sb("p_mbi", [8, 2], F32)
    p_mnbf = sb("p_mnbf", [8, 2], F32)
    PB = Buf("params")

    xres = sb("xres", [128, NTT, D], F32)
    xres_b = [Buf("xres%d" % t) for t in range(NTT)]
    xT = sb("xT", [128, KT, NT], BF16)
    xT_b = [Buf("xT%d" % t) for t in range(NTT)]
    yT = sb("yT", [128, 32, NT], BF16)
    yT_b = [Buf("yT%d" % g) for g in range(8)]
    NW = 3
    wbufs = Ring([(sb("wbuf%d" % i, [128, 16, 512], BF16), Buf("wbuf%d" % i)) for i in range(NW)])
    hst = sb("hst", [128, 2, 4096], F32)
    hst_b = [[Buf("hst%d_%d" % (j, g)) for g in range(8)] for j in range(2)]
    hbf2 = [sb("hbf%d" % i, [128, 512], BF16) for i in range(2)]
    hbf2_b = [Buf("hbf%d" % i) for i in range(2)]
    carry_s = sb("carry_s", [128, 2, 48, 3], F32)
    carry_s_b = [[Buf("cs%d_%d" % (j, g)) for g in range(10)] for j in range(2)]
    carry_m = sb("carry_m", [128, 2, 32, 3], F32)
    carry_m_b = [[Buf("cm%d_%d" % (j, h)) for h in range(8)] for j in range(2)]
    nst = sb("nst", [128, 2, 8, 4], F32)
    nst_b = [[Buf("nst%d_%d" % (j, h)) for h in range(8)] for j in range(2)]
    nbf = sb("nbf", [128, 4], BF16)
    nbf_b = Buf("nbf")
    mst = sb("mst", [8, 2], F32)
    mst_b = [Buf("mst0"), Buf("mst1")]
    Cst = Ring([(sb("Cst%d" % i, [128, 4, 512], F32), Buf("Cst%d" % i)) for i in range(2)])
    Cbf = sb("Cbf", [128, 4, 512], BF16)
    Cbf_b = Buf("Cbf")
    stg = Ring([(sb("stg%d" % i, [128, 512], F32), Buf("stg%d" % i)) for i in range(2)])
    dummies = {e: sb("dummy_" + e, [1, 4], F32) for e in ("act", "dve", "pool")}
    dummies = {e: t[:] for e, t in dummies.items()}
    dummies["zsrc"] = zeros_f[0:1, 0:4]
    dummies["zbuf"] = CB

    ARENA_B = 46 * 1024
    arena = sb("arena", [128, ARENA_B // 2], BF16)
    apos = [0]

    def areset():
        apos[0] = 0

    def aalloc(name, free_shape, dt, parts=128):
        n = 1
        for s_ in free_shape:
            n *= s_
        nb = n * (4 if dt == F32 else 2)
        nb = (nb + 3) // 4 * 4
        o = apos[0]
        apos[0] += nb
        assert apos[0] <= ARENA_B, ("arena overflow", name, apos[0])
        v = arena[0:parts, o // 2:(o + nb) // 2]
        if dt == F32:
            v = v.bitcast(F32)
        n_el = nb // (4 if dt == F32 else 2)
        v = v[:, 0:n]
        if len(free_shape) == 2:
            v = v.rearrange("p (a b) -> p a b", b=free_shape[1])
        elif len(free_shape) == 3:
            v = v.rearrange("p (a b c) -> p a b c", b=free_shape[1], c=free_shape[2])
        return v, Buf(name)

    psA = Ring([(es.enter_context(nc.psum_tensor("psA%d" % i, [128, 512], F32)), Buf("psA%d" % i, excl=True)) for i in range(6)])
    psB = Ring([(es.enter_context(nc.psum_tensor("psB%d" % i, [128, 1024], BF16)), Buf("psB%d" % i, excl=True)) for i in range(2)])

    def MM(out, lhsT, rhs, start, stop, r, w, skip=False):
        if skip:
            return S.add("pe", lambda e: e.matmul(out, lhsT=lhsT, rhs=rhs, start=start, stop=stop,
                                                  skip_group_check=True), r, w)
        return S.add("pe", lambda e: e.matmul(out, lhsT=lhsT, rhs=rhs, start=start, stop=stop), r, w)

    def TR(out, in_, ident, r, w):
        return S.add("pe", lambda e: e.transpose(out, in_, ident), r, w)

    def ACT(out, in_, func, r, w, bias=None, scale=None, accum=None):
        kw = {}
        if bias is not None:
            kw["bias"] = bias
        if scale is not None:
            kw["scale"] = scale
        if accum is not None:
            kw["accum_out"] = accum
        return S.add("act", lambda e: e.activation(out=out, in_=in_, func=func, **kw), r, w)

    def TT(eng, out, in0, in1, op, r, w):
        return S.add(eng, lambda e: e.tensor_tensor(out=out, in0=in0, in1=in1, op=op), r, w)

    def TS(eng, out, in0, s1, s2, op0, op1, r, w):
        if op1 is None:
            return S.add(eng, lambda e: e.tensor_scalar(out=out, in0=in0, scalar1=s1, scalar2=None, op0=op0), r, w)
        return S.add(eng, lambda e: e.tensor_scalar(out=out, in0=in0, scalar1=s1, scalar2=s2, op0=op0, op1=op1), r, w)

    def STT(out, in0, scalar, in1, op0, op1, r, w):
        return S.add("dve", lambda e: e.scalar_tensor_tensor(out=out, in0=in0, scalar=scalar, in1=in1,
                                                             op0=op0, op1=op1), r, w)

    def CP(eng, out, in_, r, w):
        if eng == "act":
            return S.add("act", lambda e: e.activation(out=out, in_=in_, func=AF.Copy), r, w)
        return S.add(eng, lambda e: e.tensor_copy(out=out, in_=in_), r, w)

    def MS(eng, ap, val, w):
        return S.add(eng, lambda e: e.memset(ap, val), (), w)

    def DMA(q, out, in_, r, w, slow=False, conc=False):
        if slow:
            return S.add(q, lambda e: e.dma_start(out=out, in_=in_, allow_slow_non_contiguous=True), r, w,
                         dma=True, conc=conc)
        return S.add(q, lambda e: e.dma_start(out=out, in_=in_), r, w, dma=True, conc=conc)

    def mark(name):
        if os.environ.get("KPRINT", "0") == "1":
            print("MARK", name, len(S.ops), flush=True)

    MS("pool", ones_f[:], 1.0, [CB])
    MS("pool", zeros_f[:], 0.0, [CB])
    S.add("pool", lambda e: e.affine_select(out=mask01[:], in_=ones_f[:], pattern=[[1, 128]], compare_op=ALU.is_ge,
                                            fill=0.0, base=0, channel_multiplier=-1), [CB], [CB])
    S.add("pool", lambda e: e.affine_select(out=Umat[:], in_=ones_f[:], pattern=[[-1, 128]], compare_op=ALU.is_gt,
                                            fill=0.0, base=0, channel_multiplier=1), [CB], [CB])
    S.add("pool", lambda e: e.affine_select(out=ident_f[:], in_=ones_f[:], pattern=[[-1, 128]],
                                            compare_op=ALU.is_equal, fill=0.0, base=0, channel_multiplier=1),
          [CB], [CB])
    CP("pool", ident_bf[:], ident_f[:], [CB], [CB])

    for j in range(2):
        for k in range(4):
            DMA("sp", p_scw[:, j, k, :], w_scw[j, k].rearrange("(c p) -> p c", p=128), [], [PB], slow=True, conc=True)
            DMA("sp", p_mcw[:, j, k, :], w_mcw[j, k].rearrange("(c p) -> p c", p=128), [], [PB], slow=True, conc=True)
        DMA("sp", p_scb[:, j, :], w_scb[j].rearrange("(c p) -> p c", p=128), [], [PB], slow=True, conc=True)
        DMA("sp", p_snw[:, j, :], w_snw[j].rearrange("(c p) -> p c", p=128), [], [PB], slow=True, conc=True)
        DMA("sp", p_mcb[:, j, :], w_mcb[j].rearrange("(c p) -> p c", p=128), [], [PB], slow=True, conc=True)
        DMA("sp", p_mnw[:, j, :], w_mnw[j].rearrange("h e -> (h e)").rearrange("(c p) -> p c", p=128), [], [PB], slow=True, conc=True)
        DMA("sp", p_msk[:, j, :], w_msk[j].rearrange("(c p) -> p c", p=128), [], [PB], slow=True, conc=True)
        DMA("sp", p_sdtb[:, j, :], w_sdtb[j].partition_broadcast(128), [], [PB], conc=True)
        DMA("sp", p_sA[:, j, :], w_sA[j].partition_broadcast(128), [], [PB], conc=True)
        DMA("sp", p_sD[:, j, :], w_sD[j].partition_broadcast(128), [], [PB], conc=True)
        DMA("sp", p_mbi[:, j:j + 1], w_mbi[j].rearrange("(p o) -> p o", o=1), [], [PB], slow=True, conc=True)
        DMA("sp", p_mnbf[:, j:j + 1], w_mbf[j].rearrange("(p o) -> p o", o=1), [], [PB], slow=True, conc=True)
    ACT(p_sA[:], p_sA[:], AF.Exp, [PB], [PB])
    TS("dve", p_sA[:], p_sA[:], -1.0, None, ALU.mult, None, [PB], [PB])
    TS("dve", p_mnbf[:], p_mnbf[:], -1.0, None, ALU.mult, None, [PB], [PB])

    for j in range(2):
        cvt("sdt%d" % j, c_sdt[j], w_sin[j][:, 10240:10304], 64)
        for g in range(8):
            cvt("sin%d" % j, c_sin[j, g], w_sin[j][:, g * 512:(g + 1) * 512], 512)
            cvt("sin%d" % j, c_sin[j, 8 + g], w_sin[j][:, 4096 + g * 512:4096 + (g + 1) * 512], 512)
        for g in range(8):
            dstv = c_sbc[j, g].rearrange("p (k c) -> p k c", c=256)
            for bc in range(2):
                col0 = 8192 + bc * 1024 + g * 128
                S.add("pool", (lambda o, i_: (lambda e: e.dma_start(out=o, in_=i_)))(
                    dstv[:, :, bc * 128:(bc + 1) * 128],
                    w_sin[j][:, col0:col0 + 128].rearrange("(k p) c -> p k c", p=128)),
                    w=[WB["sin%d" % j]], dma=True, conc=True)
        for dc in range(4):
            for kh in range(2):
                cvt("sout%d" % j, c_sout[j, dc * 2 + kh], w_sout[j][kh * 2048:(kh + 1) * 2048, dc * 512:(dc + 1) * 512], 512)
        cvt("mg%d" % j, c_mg[j], w_min[j][:, 12288:12304], 16)
        for h in range(8):
            for which in range(3):
                cvt("min%d" % j, c_min[j, which * 8 + h],
                    w_min[j][:, which * 4096 + h * 512:which * 4096 + (h + 1) * 512], 512)
            for qi, wsrc in enumerate((w_mq, w_mk, w_mv)):
                cvt("mqkv%d" % j, c_mqkv[j, h][:, qi * 2048:(qi + 1) * 2048], wsrc[j, h], 512)
        for dc in range(4):
            for kh in range(2):
                cvt("mout%d" % j, c_mout[j, dc * 2 + kh], w_mout[j][kh * 2048:(kh + 1) * 2048, dc * 512:(dc + 1) * 512], 512)

    def wload(pieces, wbname):
        wt, wbf = wbufs.next()
        for (dstf, src) in pieces:
            DMA("sp", dstf(wt), src, [WB[wbname]], [wbf], conc=True)
        return wt, wbf

    def wfull(chunk_ap, wbname, K=16, C=512):
        return wload([(lambda wt: wt[:, 0:K, 0:C], chunk_ap.rearrange("p (k c) -> p k c", c=C))], wbname)

    def make_xT(ntok, TL, ntt):
        for t in range(ntt):
            xb, xb_b = aalloc_xb
            CP("dve", xb[0:TL, 0:1024], xres[0:TL, t, 0:1024], [xres_b[t]], [xb_b])
            CP("act", xb[0:TL, 1024:2048], xres[0:TL, t, 1024:2048], [xres_b[t]], [xb_b])
            for q in range(4):
                ps, psb = psB.next()
                for kk in range(4):
                    kt = q * 4 + kk
                    TR(ps[:, kk * 128:kk * 128 + TL], xb[0:TL, kt * 128:(kt + 1) * 128], ident_bf[0:TL, 0:TL],
                       [xb_b, CB], [psb])
                src = ps[:, 0:512].rearrange("p (k t) -> p k t", t=128)[:, :, 0:TL]
                CP("act" if q % 2 == 0 else "dve", xT[:, q * 4:(q + 1) * 4, t * 128:t * 128 + TL], src,
                   [psb], [xT_b[t]])

    def out_proj_ln(i, bw, wbname, ntok, TL, ntt, last, ydst):
        mark("outproj")
        for dc in range(4):
            pss = [psA.next() for _ in range(ntt)]
            for kh in range(2):
                wt, wbf = wfull(bw[dc * 2 + kh], wbname)
                for t in range(ntt):
                    ps, psb = pss[t]
                    for k in range(16):
                        MM(ps[0:TL, :], yT[:, kh * 16 + k, t * 128:t * 128 + TL], wt[:, k, :],
                           (kh == 0 and k == 0), (kh == 1 and k == 15), [wbf] + yT_b, [psb])
            for t in range(ntt):
                ps, psb = pss[t]
                xs_ = xres[0:TL, t, dc * 512:(dc + 1) * 512]
                STT(xs_, xs_, ALPHA, ps[0:TL, :], ALU.mult, ALU.add, [psb, xres_b[t]], [xres_b[t]])
        mark("ln")
        lw, lwb = wbufs.next()
        lnp = lw[:].rearrange("p a b -> p (a b)").bitcast(F32).rearrange("p (a b) -> p a b", b=D)
        DMA("sp", lnp[:, 0, :], w_lng[i].partition_broadcast(128), [], [lwb], conc=True)
        DMA("sp", lnp[:, 1, :], w_lnb[i].partition_broadcast(128), [], [lwb], conc=True)
        for t in range(ntt):
            st, st_b = aalloc_st
            for q in range(4):
                S.add("dve", (lambda o, i_: (lambda e: e.bn_stats(out=o, in_=i_)))(
                    st[0:TL, q, :], xres[0:TL, t, q * 512:(q + 1) * 512]), [xres_b[t]], [st_b])
            mv, mv_b = aalloc_mv
            S.add("dve", (lambda o, i_: (lambda e: e.bn_aggr(out=o, in_=i_)))(
                mv[0:TL, 0:2], st[0:TL, :, :].rearrange("p a b -> p (a b)")), [st_b], [mv_b])
            TS("dve", mv[0:TL, 2:3], mv[0:TL, 1:2], LN_EPS, None, ALU.add, None, [mv_b], [mv_b])
            ACT(mv[0:TL, 2:3], mv[0:TL, 2:3], AF.Sqrt, [mv_b], [mv_b])
            S.add("dve", (lambda o, i_: (lambda e: e.reciprocal(out=o, in_=i_)))(mv[0:TL, 2:3], mv[0:TL, 2:3]),
                  [mv_b], [mv_b])
            xr = xres[0:TL, t, :]
            TS("dve", xr, xr, mv[0:TL, 0:1], mv[0:TL, 2:3], ALU.subtract, ALU.mult, [mv_b, xres_b[t]], [xres_b[t]])
            TT("dve", xr, xr, lnp[0:TL, 0, :], ALU.mult, [lwb, xres_b[t]], [xres_b[t]])
            TT("dve", xr, xr, lnp[0:TL, 1, :], ALU.add, [lwb, xres_b[t]], [xres_b[t]])
            if last:
                DMA("sp", ydst[t * 128:t * 128 + TL, :], xr, [xres_b[t]], [])

    xb_t = sb("xb_t", [128, D], BF16)
    aalloc_xb = (xb_t, Buf("xb_t"))
    st_t = sb("st_t", [128, 4, 6], F32)
    aalloc_st = (st_t, Buf("st_t"))
    mv_t = sb("mv_t", [128, 4], F32)
    aalloc_mv = (mv_t, Buf("mv_t"))

    def conv_silu(raw, raw_b, ntok, cw, cb_, out, out_b, tmp, tmp_b):
        TS("dve", tmp[:, 0:ntok], raw[:, 0:ntok], cw(0), cb_, ALU.mult, ALU.add, [raw_b, PB], [tmp_b])
        for k in range(1, 4):
            STT(tmp[:, 0:ntok], raw[:, k:k + ntok], cw(k), tmp[:, 0:ntok], ALU.mult, ALU.add,
                [raw_b, PB, tmp_b], [tmp_b])
        ACT(out, tmp[:, 0:ntok], AF.Silu, [tmp_b], [out_b])

    def ssd_layer(j, ntok, L, first, lastblk, seq):
        nch = ntok // L
        TL = L
        areset()
        S.fence(dummies)
        wn = "sin%d" % j
        dtv, dtv_b = aalloc("dtv", [nch, 64], F32)
        av, av_b = aalloc("av", [nch, 64], F32)
        cumv, cum_b = aalloc("cumv", [nch, 64], F32)
        ev, ev_b = aalloc("ev", [nch, 64], F32)
        tlv, tl_b = aalloc("tlv", [nch, 64], F32)
        decv, dec_b = aalloc("decv", [nch, 64], F32)
        zs, zs_b = aalloc("zs", [nch, 512], BF16)
        raw, raw_b = aalloc("raw", [6, 3 + ntok], F32)
        ctmp, ctmp_b = aalloc("ctmp", [ntok], F32)
        xc, xc_b = aalloc("xc", [6, ntok], BF16)
        xtok, xtok_b = aalloc("xtok", [512], BF16)
        btok, btok_b = aalloc("btok", [128], BF16)
        xdt, xdt_b = aalloc("xdt", [512], BF16)
        xD, xD_b = aalloc("xD", [512], BF16)
        xtl, xtl_b = aalloc("xtl", [512], BF16)
        cbm, cbm_b = aalloc("cbm", [128], F32)
        aV, aV_b = aalloc("aV", [8, 128], F32)
        eM, eM_b = aalloc("eM", [8, 128], F32)
        WT, WT_b = aalloc("WT", [8, 128], BF16)
        y1, y1_b = aalloc("y1", [512], F32)
        y2, y2_b = aalloc("y2", [512], F32)
        yn, yn_b = aalloc("yn", [512], BF16)
        ss, ss_b = aalloc("ss", [4], F32)
        htmp, htmp_b = aalloc("htmp", [512], F32)

        if first:
            if seq["kind"] == "p":
                for g in range(8):
                    MS("pool", hst[:, j, g * 512:(g + 1) * 512], 0.0, [hst_b[j][g]])
                for g in range(10):
                    lo, hi = (g * 4, g * 4 + 4) if g < 8 else (32 + (g - 8) * 8, 40 + (g - 8) * 8)
                    MS("pool", carry_s[:, j, lo:hi, :], 0.0, [carry_s_b[j][g]])
            else:
                b = seq["b"]
                src = i_sh[j, b].rearrange("h p n -> (h p) n")
                for q in range(8):
                    sg, sg_b = stg.next()
                    DMA("sp", sg[:, :].rearrange("p (a n) -> p a n", n=128),
                        src[q * 512:(q + 1) * 512, :].rearrange("(a p) n -> p a n", p=128), [], [sg_b])
                    ps, psb = psA.next()
                    for a in range(4):
                        TR(ps[:, a * 128:(a + 1) * 128], sg[:, a * 128:(a + 1) * 128], ident_f[:], [sg_b, CB], [psb])
                    CP("act", hst[:, j, q * 512:(q + 1) * 512], ps[:, :], [psb], [hst_b[j][q]])
                for q in range(12):
                    sg, sg_b = stg.next()
                    DMA("sp", sg[0:3, :], i_sc[j, b][:, q * 512:(q + 1) * 512], [], [sg_b])
                    ps, psb = psA.next()
                    for a in range(4):
                        TR(ps[:, a * 3:a * 3 + 3], sg[0:3, a * 128:(a + 1) * 128], ident_f[0:3, 0:3], [sg_b, CB], [psb])
                    g = q if q < 8 else 8 + (q - 8) // 2
                    CP("act", carry_s[:, j, q * 4:(q + 1) * 4, :],
                       ps[:, 0:12].rearrange("p (a k) -> p a k", k=3), [psb], [carry_s_b[j][g]])

        mark("ssd_dt")
        wt, wbf = wfull(c_sdt[j], "sdt%d" % j, 16, 64)
        for c in range(nch):
            ps, psb = psA.next()
            for k in range(16):
                MM(ps[0:L, 0:64], xT[:, k, c * 128:c * 128 + L], wt[:, k, 0:64], k == 0, k == 15, [wbf] + xT_b, [psb])
            TT("dve", dtv[0:L, c, :], ps[0:L, 0:64], p_sdtb[0:L, j, :], ALU.add, [psb, PB], [dtv_b])
            ACT(dtv[0:L, c, :], dtv[0:L, c, :], AF.Exp, [dtv_b], [dtv_b])
            ACT(dtv[0:L, c, :], dtv[0:L, c, :], AF.Ln, [dtv_b], [dtv_b], bias=1.0)
            TT("dve", av[0:L, c, :], dtv[0:L, c, :], p_sA[0:L, j, :], ALU.mult, [dtv_b, PB], [av_b])
            ps2, ps2b = psA.next()
            MM(ps2[0:L, 0:64], mask01[0:L, 0:L], av[0:L, c, :], True, True, [av_b, CB], [ps2b])
            MM(ps2[:, 64:128], ones_f[0:L, :], av[0:L, c, :], True, True, [av_b, CB], [ps2b])
            CP("dve", cumv[0:L, c, :], ps2[0:L, 0:64], [ps2b], [cum_b])
            ACT(ev[0:L, c, :], ps2[0:L, 0:64], AF.Exp, [ps2b], [ev_b])
            ACT(decv[:, c, :], ps2[:, 64:128], AF.Exp, [ps2b], [dec_b])
            TT("dve", tlv[0:L, c, :], ps2[0:L, 64:128], cumv[0:L, c, :], ALU.subtract, [ps2b, cum_b], [tl_b])
            ACT(tlv[0:L, c, :], tlv[0:L, c, :], AF.Exp, [tl_b], [tl_b])

        for g in range(8):
            mark("ssd_g%d_z" % g)
            hbf, hbfb = hbf2[g % 2], hbf2_b[g % 2]
            CP("pool", hbf[:, :], hst[:, j, g * 512:(g + 1) * 512], [hst_b[j][g]], [hbfb])
            wt, wbf = wfull(c_sin[j, g], wn)
            for c in range(nch):
                ps, psb = psA.next()
                for k in range(16):
                    MM(ps[0:L, :], xT[:, k, c * 128:c * 128 + L], wt[:, k, :], k == 0, k == 15, [wbf] + xT_b, [psb])
                ACT(zs[0:L, c, :], ps[0:L, :], AF.Silu, [psb], [zs_b])
            mark("ssd_g%d_x" % g)
            wt, wbf = wfull(c_sin[j, 8 + g], wn)
            for ct in range(4):
                CP("pool", raw[:, ct, 0:3], carry_s[:, j, g * 4 + ct, :], [carry_s_b[j][g]], [raw_b])
            for ct in range(4):
                ps, psb = psA.next()
                for k in range(16):
                    MM(ps[:, 0:ntok], wt[:, k, ct * 128:(ct + 1) * 128], xT[:, k, 0:ntok], k == 0, k == 15,
                       [wbf] + xT_b, [psb])
                CP("act", raw[:, ct, 3:3 + ntok], ps[:, 0:ntok], [psb], [raw_b])
            wt2, wbf2 = wfull(c_sbc[j, g], wn, 16, 256)
            for bc in range(2):
                cti = 32 + g if bc == 0 else 40 + g
                CP("pool", raw[:, 4 + bc, 0:3], carry_s[:, j, cti, :], [carry_s_b[j][8 + bc]], [raw_b])
                ps, psb = psA.next()
                for k in range(16):
                    MM(ps[:, 0:ntok], wt2[:, k, bc * 128:(bc + 1) * 128], xT[:, k, 0:ntok], k == 0, k == 15,
                       [wbf2] + xT_b, [psb])
                CP("act", raw[:, 4 + bc, 3:3 + ntok], ps[:, 0:ntok], [psb], [raw_b])
            mark("ssd_g%d_conv" % g)
            for c6 in range(6):
                cti = g * 4 + c6 if c6 < 4 else (32 + g if c6 == 4 else 40 + g)
                conv_silu(raw[:, c6, :], raw_b, ntok,
                          (lambda cti: (lambda k: p_scw[:, j, k, cti:cti + 1]))(cti), p_scb[:, j, cti:cti + 1],
                          xc[:, c6, 0:ntok], xc_b, ctmp, ctmp_b)
                cb_ = carry_s_b[j][g] if c6 < 4 else carry_s_b[j][8 + (c6 - 4)]
                CP("pool", carry_s[:, j, cti, :], raw[:, c6, ntok:ntok + 3], [raw_b], [cb_])
            for c in range(nch):
                c0 = c * L
                hs = slice(g * 8, g * 8 + 8)
                mark("ssd_g%d_c%d" % (g, c))
                ps, psb = psB.next()
                for ct in range(4):
                    TR(ps[0:L, ct * 128:(ct + 1) * 128], xc[:, ct, c0:c0 + L], ident_bf[:], [xc_b, CB], [psb])
                TR(ps[0:L, 512:640], xc[:, 4, c0:c0 + L], ident_bf[:], [xc_b, CB], [psb])
                CP("act", xtok[0:L, :], ps[0:L, 0:512], [psb], [xtok_b])
                CP("dve", btok[0:L, :], ps[0:L, 512:640], [psb], [btok_b])
                x3 = xtok[0:L, :].rearrange("p (h q) -> p h q", q=64)
                TT("dve", xdt[0:L, :].rearrange("p (h q) -> p h q", q=64), x3,
                   dtv[0:L, c, hs].unsqueeze(2).to_broadcast([L, 8, 64]), ALU.mult, [xtok_b, dtv_b], [xdt_b])
                TT("dve", xD[0:L, :].rearrange("p (h q) -> p h q", q=64), x3,
                   p_sD[0:L, j, hs].unsqueeze(2).to_broadcast([L, 8, 64]), ALU.mult, [xtok_b, PB], [xD_b])
                TT("pool", xtl[0:L, :].rearrange("p (h q) -> p h q", q=64),
                   xdt[0:L, :].rearrange("p (h q) -> p h q", q=64),
                   tlv[0:L, c, hs].unsqueeze(2).to_broadcast([L, 8, 64]), ALU.mult, [xdt_b, tl_b], [xtl_b])
                mark("ssd_cb")
                ps, psb = psA.next()
                MM(ps[0:L, 0:L], xc[:, 4, c0:c0 + L], xc[:, 5, c0:c0 + L], True, True, [xc_b], [psb])
                TT("dve", cbm[0:L, 0:L], ps[0:L, 0:L], mask01[0:L, 0:L], ALU.mult, [psb, CB], [cbm_b])
                TT("dve", aV[0:L, :, 0:L], av[0:L, c, hs].unsqueeze(2).to_broadcast([L, 8, L]),
                   mask01[0:L, 0:L].unsqueeze(1).to_broadcast([L, 8, L]), ALU.mult, [av_b, CB], [aV_b])
                hp = min(8, 512 // L)
                for hb in range(0, 8, hp):
                    ps, psb = psA.next()
                    MM(ps[0:L, 0:hp * L], Umat[0:L, 0:L], aV[0:L, hb:hb + hp, 0:L], True, True, [aV_b, CB], [psb])
                    ACT(eM[0:L, hb:hb + hp, 0:L], ps[0:L, 0:hp * L].rearrange("p (h t) -> p h t", t=L), AF.Exp,
                        [psb], [eM_b])
                TT("dve", WT[0:L, :, 0:L], eM[0:L, :, 0:L], cbm[0:L, 0:L].unsqueeze(1).to_broadcast([L, 8, L]),
                   ALU.mult, [eM_b, cbm_b], [WT_b])
                mark("ssd_y")
                ps1, ps1b = psA.next()
                MM(ps1[0:L, :], ident_bf[0:L, 0:L], xD[0:L, :], True, False, [xD_b, CB], [ps1b], skip=True)
                for hh in range(8):
                    MM(ps1[0:L, hh * 64:(hh + 1) * 64], WT[0:L, hh, 0:L], xdt[0:L, hh * 64:(hh + 1) * 64],
                       False, hh == 7, [WT_b, xdt_b], [ps1b], skip=True)
                ps2, ps2b = psA.next()
                MM(ps2[0:L, :], xc[:, 5, c0:c0 + L], hbf[:, :], True, True,
                   [xc_b, hbfb], [ps2b])
                TT("dve", y2[0:L, :].rearrange("p (h q) -> p h q", q=64),
                   ps2[0:L, :].rearrange("p (h q) -> p h q", q=64),
                   ev[0:L, c, hs].unsqueeze(2).to_broadcast([L, 8, 64]), ALU.mult, [ps2b, ev_b], [y2_b])
                TT("dve", y1[0:L, :], ps1[0:L, :], y2[0:L, :], ALU.add, [ps1b, y2_b], [y1_b])
                mark("ssd_gate")
                TT("dve", y1[0:L, :], y1[0:L, :], zs[0:L, c, :], ALU.mult, [y1_b, zs_b], [y1_b])
                ACT(y2[0:L, :], y1[0:L, :], AF.Square, [y1_b], [y2_b, ss_b], accum=ss[0:L, 0:1])
                TS("dve", ss[0:L, 1:2], ss[0:L, 0:1], 1.0 / 512, RMS_EPS, ALU.mult, ALU.add, [ss_b], [ss_b])
                ACT(ss[0:L, 1:2], ss[0:L, 1:2], AF.Sqrt, [ss_b], [ss_b])
                S.add("dve", (lambda o, i_: (lambda e: e.reciprocal(out=o, in_=i_)))(ss[0:L, 2:3], ss[0:L, 1:2]),
                      [ss_b], [ss_b])
                ACT(yn[0:L, :], y1[0:L, :], AF.Copy, [y1_b, ss_b], [yn_b], scale=ss[0:L, 2:3])
                mark("ssd_back")
                ps, psb = psB.next()
                for ct in range(4):
                    TR(ps[:, ct * 128:ct * 128 + L], yn[0:L, ct * 128:(ct + 1) * 128], ident_bf[0:L, 0:L],
                       [yn_b, CB], [psb])
                for ct in range(4):
                    ACT(yT[:, g * 4 + ct, c0:c0 + L], ps[:, ct * 128:ct * 128 + L], AF.Copy, [psb, PB], [yT_b[g]],
                        scale=p_snw[:, j, g * 4 + ct:g * 4 + ct + 1])
                mark("ssd_state")
                ps, psb = psA.next()
                MM(ps[:, :], btok[0:L, :], xtl[0:L, :], True, True, [btok_b, xtl_b], [psb])
                hv = hst[:, j, g * 512:(g + 1) * 512]
                TT("dve", htmp[:, :].rearrange("p (h q) -> p h q", q=64), hv.rearrange("p (h q) -> p h q", q=64),
                   decv[:, c, hs].unsqueeze(2).to_broadcast([128, 8, 64]), ALU.mult, [hst_b[j][g], dec_b], [htmp_b])
                TT("dve", hv, htmp[:, :], ps[:, :], ALU.add, [htmp_b, psb], [hst_b[j][g]])
                if c < nch - 1:
                    CP("pool", hbf[:, :], hv, [hst_b[j][g]], [hbfb])

        mark("ssd_out")
        if lastblk:
            kind = 0 if seq["kind"] == "p" else 1
            dsth = o_sh[kind][j] if kind == 0 else o_sh[kind][j, seq["b"]]
            dstc = o_sc[kind][j] if kind == 0 else o_sc[kind][j, seq["b"]]
            dh = dsth.rearrange("h p n -> (h p) n")
            for q in range(8):
                ps, psb = psA.next()
                for a in range(4):
                    TR(ps[:, a * 128:(a + 1) * 128], hst[:, j, q * 512 + a * 128:q * 512 + (a + 1) * 128], ident_f[:],
                       [hst_b[j][q], CB], [psb])
                sg, sg_b = stg.next()
                CP("act", sg[:, :], ps[:, :], [psb], [sg_b])
                DMA("sp", dh[q * 512:(q + 1) * 512, :].rearrange("(a p) n -> p a n", p=128),
                    sg[:, :].rearrange("p (a n) -> p a n", n=128), [sg_b], [])
            for q in range(12):
                ps, psb = psA.next()
                g = q if q < 8 else 8 + (q - 8) // 2
                for a in range(4):
                    TR(ps[0:3, a * 128:(a + 1) * 128], carry_s[:, j, q * 4 + a, :], ident_f[:],
                       [carry_s_b[j][g], CB], [psb])
                sg, sg_b = stg.next()
                CP("act", sg[0:3, :], ps[0:3, :], [psb], [sg_b])
                DMA("sp", dstc[:, q * 512:(q + 1) * 512], sg[0:3, :], [sg_b], [])

    def ml_layer(j, ntok, L, first, lastblk, seq):
        nch = ntok // L
        areset()
        S.fence(dummies)
        wn = "min%d" % j
        li, li_b = aalloc("li", [ntok], F32, parts=8)
        lf, lf_b = aalloc("lf", [ntok], F32, parts=8)
        bb, bb_b = aalloc("bb", [ntok], F32, parts=8)
        uu, uu_b = aalloc("uu", [ntok], F32, parts=8)
        MR, MR_b = aalloc("MR", [ntok], F32, parts=8)
        rows, rows_b = aalloc("rows", [4, ntok], F32, parts=8)
        sc, sc_b = aalloc("sc", [nch, 4], F32, parts=8)
        gtok, gtok_b = aalloc("gtok", [nch, 4, 8], F32)
        tlbf, tlbf_b = aalloc("tlbf", [nch, 8], BF16)
        decb, decb_b = aalloc("decb", [nch, 8], F32)
        dg, dg_b = aalloc("dg", [8], F32, parts=8)
        raw, raw_b = aalloc("raw", [4, 3 + ntok], F32)
        ctmp, ctmp_b = aalloc("ctmp", [ntok], F32)
        xmb, xmb_b = aalloc("xmb", [4, ntok], BF16)
        xc, xc_b = aalloc("xc", [4, ntok], BF16)
        zs, zs_b = aalloc("zs", [4, ntok], BF16)
        so, so_b = aalloc("so", [4, ntok], BF16)
        qT, qT_b = aalloc("qT", [4, ntok], BF16)
        kT, kT_b = aalloc("kT", [4, ntok], BF16)
        vt, vt_b = aalloc("vt", [nch, 512], BF16)
        ktok, ktok_b = aalloc("ktok", [nch, 512], BF16)
        hnT, hnT_b = aalloc("hnT", [4, ntok], BF16)
        qkm, qkm_b = aalloc("qkm", [128], BF16)
        Asb, Asb_b = aalloc("Asb", [512], F32)
        num, num_b = aalloc("num", [512], F32)
        hn, hn_b = aalloc("hn", [512], BF16)
        dn, dn_b = aalloc("dn", [8], F32)
        st6, st6_b = aalloc("st6", [6], F32)
        t1, t1_b = aalloc("t1", [ntok], F32)

        kind = 0 if seq["kind"] == "p" else 1
        if first:
            if kind == 0:
                for h in range(8):
                    MS("pool", nst[:, j, h, :], 0.0, [nst_b[j][h]])
                    MS("pool", carry_m[:, j, h * 4:(h + 1) * 4, :], 0.0, [carry_m_b[j][h]])
                MS("pool", mst[:, j:j + 1], 0.0, [mst_b[j]])
            else:
                b = seq["b"]
                for h in range(8):
                    DMA("sp", nst[:, j, h, :], i_mn[j, b, h].rearrange("(e p) -> p e", p=128), [], [nst_b[j][h]],
                        slow=True)
                DMA("sp", mst[:, j:j + 1], i_mm[j, b].rearrange("(p o) -> p o", o=1), [], [mst_b[j]], slow=True)
                for q in range(8):
                    sg, sg_b = stg.next()
                    DMA("sp", sg[0:3, :], i_mc[j, b][:, q * 512:(q + 1) * 512], [], [sg_b])
                    ps, psb = psA.next()
                    for a in range(4):
                        TR(ps[:, a * 3:a * 3 + 3], sg[0:3, a * 128:(a + 1) * 128], ident_f[0:3, 0:3], [sg_b, CB], [psb])
                    CP("act", carry_m[:, j, q * 4:(q + 1) * 4, :],
                       ps[:, 0:12].rearrange("p (a k) -> p a k", k=3), [psb], [carry_m_b[j][q]])

        wt, wbf = wfull(c_mg[j], "mg%d" % j, 16, 16)
        ps, psb = psA.next()
        for k in range(16):
            MM(ps[0:8, 0:ntok], wt[:, k, 0:8], xT[:, k, 0:ntok], k == 0, k == 15, [wbf] + xT_b, [psb])
        ACT(li[:, :], ps[0:8, 0:ntok], AF.Identity, [psb, PB], [li_b], bias=p_mbi[:, j:j + 1])
        ps, psb = psA.next()
        for k in range(16):
            MM(ps[0:8, 0:ntok], wt[:, k, 8:16], xT[:, k, 0:ntok], k == 0, k == 15, [wbf] + xT_b, [psb])
        ACT(lf[:, :], ps[0:8, 0:ntok], AF.Exp, [psb, PB], [lf_b], bias=p_mnbf[:, j:j + 1], scale=-1.0)
        ACT(lf[:, :], lf[:, :], AF.Ln, [lf_b], [lf_b], bias=1.0)
        TS("dve", lf[:, :], lf[:, :], -1.0, None, ALU.mult, None, [lf_b], [lf_b])
        for c in range(nch):
            cs = slice(c * L, (c + 1) * L)
            mprev = mst[:, j:j + 1]
            S.add("dve", (lambda o, d0, d1: (lambda e: e.tensor_tensor_scan(
                out=o, data0=d0, data1=d1, initial=0.0, op0=ALU.add, op1=ALU.add)))(bb[:, cs], lf[:, cs], zeros_f[0:8, 0:L]),
                [lf_b, CB], [bb_b])
            TT("dve", uu[:, cs], li[:, cs], bb[:, cs], ALU.subtract, [li_b, bb_b], [uu_b])
            S.add("dve", (lambda o, d0, ini: (lambda e: e.tensor_tensor_scan(
                out=o, data0=d0, data1=d0, initial=ini, op0=ALU.max, op1=ALU.max)))(MR[:, cs], uu[:, cs], mprev),
                [uu_b, mst_b[j]], [MR_b])
            last1 = slice((c + 1) * L - 1, (c + 1) * L)
            TS("dve", sc[:, c, 0:1], MR[:, last1], -1.0, None, ALU.mult, None, [MR_b], [sc_b])
            CP("dve", sc[:, c, 1:2], MR[:, last1], [MR_b], [sc_b])
            CP("dve", sc[:, c, 2:3], mprev, [mst_b[j]], [sc_b])
            ACT(rows[:, 0, cs], uu[:, cs], AF.Exp, [uu_b, sc_b], [rows_b], bias=sc[:, c, 0:1])
            ACT(rows[:, 1, cs], MR[:, cs], AF.Exp, [MR_b, sc_b], [rows_b], bias=sc[:, c, 1:2], scale=-1.0)
            ACT(rows[:, 2, cs], MR[:, cs], AF.Exp, [MR_b, sc_b], [rows_b], bias=sc[:, c, 2:3], scale=-1.0)
            TT("dve", rows[:, 3, cs], bb[:, cs], MR[:, cs], ALU.add, [bb_b, MR_b], [rows_b])
            CP("dve", mst[:, j:j + 1], rows[:, 3, last1], [rows_b, sc_b], [mst_b[j]])
            ACT(rows[:, 3, cs], rows[:, 3, cs], AF.Exp, [rows_b, mst_b[j]], [rows_b], scale=-1.0)
            ps, psb = psA.next()
            for a in range(4):
                TR(ps[0:L, a * 8:(a + 1) * 8], rows[:, a, cs], ident_f[0:8, 0:8], [rows_b, CB], [psb])
            CP("dve", gtok[0:L, c, :, :], ps[0:L, 0:32].rearrange("p (a h) -> p a h", h=8), [psb], [gtok_b])
            CP("pool", tlbf[0:L, c, :], gtok[0:L, c, 0, :], [gtok_b], [tlbf_b])
            TS("dve", dg[:, :], ident_f[0:8, 0:8], rows[:, 2, last1], None, ALU.mult, None, [rows_b, CB], [dg_b])
            ps, psb = psA.next()
            MM(ps[:, 0:8], ones_f[0:8, :], dg[:, :], True, True, [dg_b, CB], [psb])
            CP("dve", decb[:, c, :], ps[:, 0:8], [psb], [decb_b])

        pend = []
        for h in range(8):
            Ct, Ct_b = Cst.next()
            if first:
                if kind == 0:
                    MS("pool", Ct[:], 0.0, [Ct_b])
                else:
                    DMA("sp", Ct[:], i_mC[j, seq["b"], h].rearrange("(k p) e -> p k e", p=128), [], [Ct_b])
            else:
                DMA("sp", Ct[:], c_scr[j, h].rearrange("(k p) e -> p k e", p=128), [CSB[j][h]], [Ct_b])
            CP("act", Cbf[:], Ct[:], [Ct_b], [Cbf_b])
            CP("pool", nbf[:, :], nst[:, j, h, :], [nst_b[j][h]], [nbf_b])
            for which in range(3):
                wt, wbf = wfull(c_min[j, which * 8 + h], wn)
                if which == 0:
                    while pend:
                        a_ = pend.pop(0)
                        DMA("sp", a_[0], a_[1], a_[2], a_[3])
                if which == 0:
                    for ct in range(4):
                        CP("pool", raw[:, ct, 0:3], carry_m[:, j, h * 4 + ct, :], [carry_m_b[j][h]], [raw_b])
                for ct in range(4):
                    ps, psb = psA.next()
                    for k in range(16):
                        MM(ps[:, 0:ntok], wt[:, k, ct * 128:(ct + 1) * 128], xT[:, k, 0:ntok], k == 0, k == 15,
                           [wbf] + xT_b, [psb])
                    if which == 0:
                        CP("act", raw[:, ct, 3:3 + ntok], ps[:, 0:ntok], [psb], [raw_b])
                        CP("dve", xmb[:, ct, 0:ntok], ps[:, 0:ntok], [psb], [xmb_b])
                    elif which == 1:
                        ACT(zs[:, ct, 0:ntok], ps[:, 0:ntok], AF.Silu, [psb], [zs_b])
                    else:
                        ACT(so[:, ct, 0:ntok], ps[:, 0:ntok], AF.Sigmoid, [psb], [so_b])
            for ct in range(4):
                cti = h * 4 + ct
                conv_silu(raw[:, ct, :], raw_b, ntok,
                          (lambda cti: (lambda k: p_mcw[:, j, k, cti:cti + 1]))(cti), p_mcb[:, j, cti:cti + 1],
                          xc[:, ct, 0:ntok], xc_b, ctmp, ctmp_b)
                CP("pool", carry_m[:, j, cti, :], raw[:, ct, ntok:ntok + 3], [raw_b], [carry_m_b[j][h]])
            wt, wbf = wfull(c_mqkv[j, h], "mqkv%d" % j, 12, 512)
            for et in range(4):
                ps, psb = psA.next()
                for k in range(4):
                    MM(ps[:, 0:ntok], wt[:, k, et * 128:(et + 1) * 128], xc[:, k, 0:ntok], k == 0, k == 3,
                       [wbf, xc_b], [psb])
                ACT(qT[:, et, 0:ntok], ps[:, 0:ntok], AF.Copy, [psb], [qT_b], scale=float(512 ** -0.5))
                ps, psb = psA.next()
                for k in range(4):
                    MM(ps[:, 0:ntok], wt[:, 4 + k, et * 128:(et + 1) * 128], xc[:, k, 0:ntok], k == 0, k == 3,
                       [wbf, xc_b], [psb])
                CP("dve", kT[:, et, 0:ntok], ps[:, 0:ntok], [psb], [kT_b])
            for c in range(nch):
                cs = slice(c * L, (c + 1) * L)
                ps, psb = psA.next()
                for k in range(4):
                    MM(ps[0:L, :], xmb[:, k, cs], wt[:, 8 + k, :], k == 0, k == 3, [wbf, xmb_b], [psb])
                ACT(vt[0:L, c, :], ps[0:L, :], AF.Copy, [psb, gtok_b], [vt_b], scale=gtok[0:L, c, 0, h:h + 1])
                ps, psb = psA.next()
                for k in range(4):
                    MM(ps[0:L, :], xc[:, k, cs], wt[:, 4 + k, :], k == 0, k == 3, [wbf, xc_b], [psb])
                CP("dve", ktok[0:L, c, :], ps[0:L, :], [psb], [ktok_b])
            for c in range(nch):
                cs = slice(c * L, (c + 1) * L)
                ps, psb = psA.next()
                for k in range(4):
                    MM(ps[0:L, 0:L], kT[:, k, cs], qT[:, k, cs], k == 0, k == 3, [kT_b, qT_b], [psb])
                TT("dve", qkm[0:L, 0:L], ps[0:L, 0:L], mask01[0:L, 0:L], ALU.mult, [psb, CB], [qkm_b])
                psa, psab = psA.next()
                MM(psa[0:L, :], qkm[0:L, 0:L], vt[0:L, c, :], True, True, [qkm_b, vt_b], [psab])
                psb2, psb2b = psA.next()
                for k in range(4):
                    MM(psb2[0:L, :], qT[:, k, cs], Cbf[:, k, :], k == 0, k == 3, [qT_b, Cbf_b], [psb2b])
                psd, psdb = psA.next()
                MM(psd[0:L, 0:1], qkm[0:L, 0:L], tlbf[0:L, c, h:h + 1], True, True, [qkm_b, tlbf_b], [psdb])
                for k in range(4):
                    MM(psd[0:L, 1:2], qT[:, k, cs], nbf[:, k:k + 1], k == 0, k == 3, [qT_b, nbf_b], [psdb])
                ACT(Asb[0:L, :], psa[0:L, :], AF.Copy, [psab, gtok_b], [Asb_b], scale=gtok[0:L, c, 1, h:h + 1])
                STT(num[0:L, :], psb2[0:L, :], gtok[0:L, c, 2, h:h + 1], Asb[0:L, :], ALU.mult, ALU.add,
                    [psb2b, gtok_b, Asb_b], [num_b])
                TT("dve", dn[0:L, 0:2], psd[0:L, 0:2], gtok[0:L, c, 1:3, h], ALU.mult, [psdb, gtok_b], [dn_b])
                TT("dve", dn[0:L, 2:3], dn[0:L, 0:1], dn[0:L, 1:2], ALU.add, [dn_b], [dn_b])
                TS("dve", dn[0:L, 3:4], dn[0:L, 2:3], -1.0, None, ALU.mult, None, [dn_b], [dn_b])
                TT("dve", dn[0:L, 3:4], dn[0:L, 3:4], dn[0:L, 2:3], ALU.max, [dn_b], [dn_b])
                TT("dve", dn[0:L, 4:5], dn[0:L, 3:4], gtok[0:L, c, 3, h:h + 1], ALU.max, [dn_b, gtok_b], [dn_b])
                S.add("dve", (lambda o, i_: (lambda e: e.reciprocal(out=o, in_=i_)))(dn[0:L, 5:6], dn[0:L, 4:5]),
                      [dn_b], [dn_b])
                TS("dve", num[0:L, :], num[0:L, :], dn[0:L, 5:6], None, ALU.mult, None, [num_b, dn_b], [num_b])
                S.add("dve", (lambda o, i_: (lambda e: e.bn_stats(out=o, in_=i_)))(st6[0:L, 0:6], num[0:L, :]),
                      [num_b], [st6_b])
                S.add("dve", (lambda o, i_: (lambda e: e.bn_aggr(out=o, in_=i_)))(dn[0:L, 6:8], st6[0:L, 0:6]),
                      [st6_b], [dn_b])
                TS("dve", dn[0:L, 7:8], dn[0:L, 7:8], LN_EPS, None, ALU.add, None, [dn_b], [dn_b])
                ACT(dn[0:L, 7:8], dn[0:L, 7:8], AF.Sqrt, [dn_b], [dn_b])
                S.add("dve", (lambda o, i_: (lambda e: e.reciprocal(out=o, in_=i_)))(dn[0:L, 7:8], dn[0:L, 7:8]),
                      [dn_b], [dn_b])
                TS("dve", hn[0:L, :], num[0:L, :], dn[0:L, 6:7], dn[0:L, 7:8], ALU.subtract, ALU.mult,
                   [num_b, dn_b], [hn_b])
                ps, psb = psB.next()
                for et in range(4):
                    TR(ps[:, et * 128:et * 128 + L], hn[0:L, et * 128:(et + 1) * 128], ident_bf[0:L, 0:L],
                       [hn_b, CB], [psb])
                CP("act", hnT[:, :, cs], ps[:, 0:512].rearrange("p (a t) -> p a t", t=128)[:, :, 0:L], [psb], [hnT_b])
                psn, psnb = psA.next()
                for kt_ in range(4):
                    MM(psn[:, kt_:kt_ + 1], ktok[0:L, c, kt_ * 128:(kt_ + 1) * 128], tlbf[0:L, c, h:h + 1], True, True,
                       [ktok_b, tlbf_b], [psnb])
                STT(nst[:, j, h, :], nst[:, j, h, :], decb[:, c, h:h + 1], psn[:, 0:4], ALU.mult, ALU.add,
                    [psnb, decb_b, nst_b[j][h]], [nst_b[j][h]])
                CP("pool", nbf[:, :], nst[:, j, h, :], [nst_b[j][h]], [nbf_b])
                for kt_ in range(4):
                    ps, psb = psA.next()
                    MM(ps[:, :], ktok[0:L, c, kt_ * 128:(kt_ + 1) * 128], vt[0:L, c, :], True, True,
                       [ktok_b, vt_b], [psb])
                    STT(Ct[:, kt_, :], Ct[:, kt_, :], decb[:, c, h:h + 1], ps[:, :], ALU.mult, ALU.add,
                        [psb, decb_b, Ct_b], [Ct_b])
                if c < nch - 1:
                    CP("act", Cbf[:], Ct[:], [Ct_b], [Cbf_b])
            if lastblk:
                dstC = o_mC[kind][j, h] if kind == 0 else o_mC[kind][j, seq["b"], h]
                pend.append((dstC.rearrange("(k p) e -> p k e", p=128), Ct[:], [Ct_b], []))
            else:
                pend.append((c_scr[j, h].rearrange("(k p) e -> p k e", p=128), Ct[:], [Ct_b], [CSB[j][h]]))
            for et in range(4):
                cti = h * 4 + et
                STT(t1[:, 0:ntok], hnT[:, et, 0:ntok], p_mnw[:, j, cti:cti + 1], so[:, et, 0:ntok], ALU.mult, ALU.mult,
                    [hnT_b, so_b, PB], [t1_b])
                STT(t1[:, 0:ntok], xc[:, et, 0:ntok], p_msk[:, j, cti:cti + 1], t1[:, 0:ntok], ALU.mult, ALU.add,
                    [xc_b, t1_b, PB], [t1_b])
                TT("dve", yT[:, cti, 0:ntok], t1[:, 0:ntok], zs[:, et, 0:ntok], ALU.mult, [t1_b, zs_b], [yT_b[h]])

        while pend:
            a_ = pend.pop(0)
            DMA("sp", a_[0], a_[1], a_[2], a_[3])
        if lastblk:
            dn_ = o_mn[kind][j] if kind == 0 else o_mn[kind][j, seq["b"]]
            dm_ = o_mm[kind][j] if kind == 0 else o_mm[kind][j, seq["b"]]
            dc_ = o_mc[kind][j] if kind == 0 else o_mc[kind][j, seq["b"]]
            for h in range(8):
                DMA("sp", dn_[h].rearrange("(e p) -> p e", p=128), nst[:, j, h, :], [nst_b[j][h]], [], slow=True)
            DMA("sp", dm_.rearrange("(p o) -> p o", o=1), mst[:, j:j + 1], [mst_b[j]], [], slow=True)
            for q in range(8):
                ps, psb = psA.next()
                for a in range(4):
                    TR(ps[0:3, a * 128:(a + 1) * 128], carry_m[:, j, q * 4 + a, :], ident_f[:],
                       [carry_m_b[j][q], CB], [psb])
                sg, sg_b = stg.next()
                CP("act", sg[0:3, :], ps[0:3, :], [psb], [sg_b])
                DMA("sp", dc_[:, q * 512:(q + 1) * 512], sg[0:3, :], [sg_b], [])

    CSB = [[Buf("cscr%d_%d" % (j, h)) for h in range(8)] for j in range(2)]
    qk_all = Buf("qkv_all")

    seqs = []
    if NPB > 0:
        seqs.append({"kind": "p", "T": TP, "ntok": NT, "L": 128, "nblk": NPB})
    for b in range(NS):
        seqs.append({"kind": "s", "b": b, "T": 16, "ntok": 16, "L": 16, "nblk": 1})
    import os as _os
    _dbg = _os.environ.get("KDBG", "")
    if _dbg == "conv":
        seqs = []
    if _os.environ.get("KNOSAMPLE", "0") == "1":
        seqs = [q for q in seqs if q["kind"] == "p"]
    if _os.environ.get("KNOPROMPT", "0") == "1":
        seqs = [q for q in seqs if q["kind"] == "s"]
    for seq in seqs:
        ntok, L = seq["ntok"], seq["L"]
        TL = min(128, ntok)
        ntt = max(1, ntok // 128)
        for blk in range(seq["nblk"]):
            first = blk == 0
            lastblk = blk == seq["nblk"] - 1
            if seq["kind"] == "p":
                xsrc = xp[blk * NT:(blk + 1) * NT, :]
                ydst = yp[blk * NT:(blk + 1) * NT, :]
            else:
                xsrc = xs[seq["b"]]
                ydst = ys[seq["b"]]
            for t in range(ntt):
                DMA("sp", xres[0:TL, t, :], xsrc[t * 128:t * 128 + TL, :], [], [xres_b[t]])
            _nl = int(_os.environ.get("KLAYERS", "4"))
            if _nl < DEPTH:
                if _nl == 0 and _os.environ.get("KXT", "0") == "1":
                    make_xT(ntok, TL, ntt)
                for t in range(ntt):
                    DMA("sp", ydst[t * 128:t * 128 + TL, :], xres[0:TL, t, :], [xres_b[t]] + xT_b, [])
            for i in range(_nl):
                j = i // 2
                make_xT(ntok, TL, ntt)
                if i % 2 == 0:
                    ssd_layer(j, ntok, L, first, lastblk, seq)
                    out_proj_ln(i, c_sout[j], "sout%d" % j, ntok, TL, ntt, i == _nl - 1, ydst)
                else:
                    ml_layer(j, ntok, L, first, lastblk, seq)
                    out_proj_ln(i, c_mout[j], "mout%d" % j, ntok, TL, ntt, i == _nl - 1, ydst)

    S.emit(nc, es)
    es.close()
    return nc, len(S.ops)


_CACHE = {}


def _get_program(NPB, NS, NT):
    key = (NPB, NS, NT)
    if key not in _CACHE:
        _CACHE[key] = build(NPB, NS, NT)
    return _CACHE[key]


WEIGHT_KEYS = ["ssd_w_in", "ssd_conv_w", "ssd_conv_b", "ssd_dt_bias", "ssd_A_log", "ssd_D", "ssd_norm_w", "ssd_w_out",
               "ml_w_in", "ml_conv_w", "ml_conv_b", "ml_w_q", "ml_w_k", "ml_w_v", "ml_b_i", "ml_b_f", "ml_norm_w",
               "ml_skip", "ml_w_out", "ln_g", "ln_b"]


def kernel(NT=256, **inp):
    f = lambda a: np.ascontiguousarray(np.asarray(a, dtype=np.float32))
    xp_all = f(inp["x_prompt"])
    xs_all = f(inp["x_sample"])
    B, T, _ = xp_all.shape
    DB = xs_all.shape[0]
    assert DB % NCORES == 0 and T % NT == 0 and B <= NCORES
    NS = DB // NCORES
    NPB = T // NT
    nc, nops = _get_program(NPB, NS, NT)
    wts = {k: f(inp[k]) for k in WEIGHT_KEYS}
    st = {k: f(inp[k]) for k in ["state_ssd_conv", "state_ssd_h", "state_mlstm_conv", "state_mlstm_C",
                                 "state_mlstm_n", "state_mlstm_m"]}
    in_maps = []
    pcores = [0, 1, 4, 5][:B] if (B <= 4 and NCORES == 8) else list(range(B))
    zero_p = np.zeros_like(xp_all[0])
    for c in range(NCORES):
        sl = slice(c * NS, (c + 1) * NS)
        m = {"xp": (xp_all[pcores.index(c)] if c in pcores else zero_p), "xs": xs_all[sl],
             "i_sc": np.ascontiguousarray(st["state_ssd_conv"][:, sl]),
             "i_sh": np.ascontiguousarray(st["state_ssd_h"][:, sl]),
             "i_mc": np.ascontiguousarray(st["state_mlstm_conv"][:, sl]),
             "i_mC": np.ascontiguousarray(st["state_mlstm_C"][:, sl]),
             "i_mn": np.ascontiguousarray(st["state_mlstm_n"][:, sl]),
             "i_mm": np.ascontiguousarray(st["state_mlstm_m"][:, sl])}
        m.update(wts)
        in_maps.append(m)
    if os.environ.get("KTRACE", "0") == "1":
        res = run_bass_kernel_spmd(nc, in_maps, core_ids=list(range(NCORES)), trace=True)
        print("EXEC_TIME_NS", res.exec_time_ns, flush=True)
        try:
            import collections
            insts = res.instructions_and_trace[0]
            t0 = min(i.timestamp for i in insts); t1 = max(i.end_timestamp for i in insts)
            lo = t0 + (t1 - t0) * float(os.environ.get("KWIN0", "0.6")); hi = t0 + (t1 - t0) * float(os.environ.get("KWIN1", "0.95"))
            print("window ms", (hi - lo) / 1e6)
            agg = collections.defaultdict(lambda: [0, 0])
            aggl = collections.defaultdict(lambda: [0, 0])
            for i in insts:
                if i.timestamp < lo or i.timestamp > hi:
                    continue
                k = (i.engine, i.name)
                agg[k][0] += i.duration; agg[k][1] += 1
                kl = (i.engine, i.name, i.source_line)
                aggl[kl][0] += i.duration; aggl[kl][1] += 1
            for k, v in sorted(agg.items(), key=lambda kv: -kv[1][0])[:28]:
                print("AGG %-10s %-28s %9.1f us n=%d" % (k[0], k[1], v[0] / 1e3, v[1]))
            for k, v in sorted(aggl.items(), key=lambda kv: -kv[1][0])[:40]:
                print("LINE %-10s %-24s L%-5s %9.1f us n=%d" % (k[0], k[1], k[2], v[0] / 1e3, v[1]))
        except Exception as e_:
            print("trace agg failed", e_)
    else:
        res = run_bass_kernel_spmd(nc, in_maps, core_ids=list(range(NCORES)))
    R = res.results
    y_prompt = np.stack([R[c]["yp"] for c in pcores], 0)
    y_sample = np.concatenate([R[c]["ys"] for c in range(NCORES)], 0)

    def pst(name):
        return np.stack([R[c][name] for c in pcores], 1)

    def sst(name):
        return np.concatenate([R[c][name] for c in range(NCORES)], 1)

    outs = (y_prompt, y_sample, pst("p_sc"), pst("p_sh"), pst("p_mc"), pst("p_mC"), pst("p_mn"), pst("p_mm"),
            sst("s_sc"), sst("s_sh"), sst("s_mc"), sst("s_mC"), sst("s_mn"), sst("s_mm"))
    return tuple(np.ascontiguousarray(o, dtype=np.float32) for o in outs)
```
